# Optimizing a Trainium2 kernel written in Bass

```python
import jax, jax.numpy as jnp
from jax import lax
import numpy as np

D_MODEL = 1024
BATCH = 32
SEQ = 2048
DEPTH = 4

CHUNK = 64
N_MIXERS = 4
POOL_WINDOWS = (2, 4, 8, 16)
GMLP_CHUNK = 128
GMLP_WIDTH = 2 * D_MODEL
GMLP_HEADS = 8
CONV_WIDTH = 31
SHORT_CONV_WIDTH = 3
D_FF = 2816
FFN_RES_WEIGHT = 0.5
DEEPNORM_ALPHA = (2 * DEPTH) ** 0.25
DEEPNORM_BETA = (8 * DEPTH) ** -0.25
LN_EPS = 1e-5

kernel_name = "hybrid_interleaved_pool_gmlp_conv_shortconv_deepnorm"


def _layer_norm(x, g, b):
    xf = x.astype(jnp.float32)
    mu = jnp.mean(xf, axis=-1, keepdims=True)
    var = jnp.mean(jnp.square(xf - mu), axis=-1, keepdims=True)
    return ((xf - mu) * lax.rsqrt(var + LN_EPS) * g.astype(jnp.float32)
            + b.astype(jnp.float32)).astype(x.dtype)


def _causal_depthwise_conv(x, w):
    k, c = w.shape
    return lax.conv_general_dilated(
        x, w[:, None, :].astype(x.dtype), window_strides=(1,),
        padding=[(k - 1, 0)], dimension_numbers=('NWC', 'WIO', 'NWC'),
        feature_group_count=c)


def _swiglu(x, w_in, w_out):
    gate, up = jnp.split(x @ w_in, 2, axis=-1)
    return (jax.nn.silu(gate) * up) @ w_out


def _pool_mixer(x, w_grp, scale):
    b, s, d = x.shape
    n_g = len(POOL_WINDOWS)
    xg = x.reshape(b, s, n_g, d // n_g)
    t = jnp.arange(s)
    outs = []
    for gi, win in enumerate(POOL_WINDOWS):
        xi = xg[:, :, gi, :].astype(jnp.float32)
        cs = jnp.cumsum(xi, axis=1)
        lagged = jnp.pad(cs, ((0, 0), (win, 0), (0, 0)))[:, :s]
        cnt = jnp.minimum(t + 1, win).astype(jnp.float32)[None, :, None]
        outs.append((cs - lagged) / cnt - xi)
    p = jnp.stack(outs, axis=2).astype(x.dtype)
    y = jnp.einsum('bsgc,gcd->bsgd', p, w_grp).reshape(b, s, d)
    return y * scale


def _gmlp_mixer(x, w_in, v_ln_g, v_ln_b, ws, bs, w_out):
    b, s, _ = x.shape
    z = jax.nn.gelu(x @ w_in)
    u, v = jnp.split(z, 2, axis=-1)
    v = _layer_norm(v, v_ln_g, v_ln_b)
    n_h, l, _ = ws.shape
    e = v.shape[-1]
    mask = jnp.tril(jnp.ones((l, l), dtype=bool))
    w_masked = jnp.where(mask[None], ws, jnp.zeros_like(ws)).astype(v.dtype)
    vc = v.reshape(b, s // l, l, n_h, e // n_h)
    sv = jnp.einsum('hts,bcshd->bcthd', w_masked, vc) + bs.T[None, None, :, :, None]
    return (u * sv.reshape(b, s, e)) @ w_out


def _conformer_conv(x, w_in, b_in, dw, dw_b, ln_g, ln_b, w_out, b_out):
    a, g = jnp.split(x @ w_in + b_in, 2, axis=-1)
    h = a * jax.nn.sigmoid(g)
    h = _causal_depthwise_conv(h, dw) + dw_b
    h = jax.nn.silu(_layer_norm(h, ln_g, ln_b))
    return h @ w_out + b_out


def _short_conv(x, w_in, k, w_out):
    b_gate, c_gate, h = jnp.split(x @ w_in, 3, axis=-1)
    return (b_gate * _causal_depthwise_conv(c_gate * h, k)) @ w_out


def setup_inputs(seed: int = 0) -> dict:
    key = jax.random.key(seed)
    keys = iter(jax.random.split(key, 32))
    d = D_MODEL
    n_per = [len(range(m, DEPTH, N_MIXERS)) for m in range(N_MIXERS)]
    n_a, n_b, n_c, n_d = n_per
    cg = d // len(POOL_WINDOWS)

    def nrm(shape, scale):
        return jax.random.normal(next(keys), shape, jnp.float32) * scale

    return {
        "x": nrm((BATCH, SEQ, d), 1.0),
        "ln_g": 1.0 + nrm((DEPTH, 3, d), 0.02),
        "ln_b": nrm((DEPTH, 3, d), 0.02),
        "ffn_w_in": nrm((DEPTH, 2, d, 2 * D_FF), d ** -0.5),
        "ffn_w_out": nrm((DEPTH, 2, D_FF, d), D_FF ** -0.5 * DEEPNORM_BETA),
        "pool_w": nrm((n_a, len(POOL_WINDOWS), cg, cg), cg ** -0.5 * DEEPNORM_BETA),
        "pool_scale": 1.0 + nrm((n_a, d), 0.1),
        "gmlp_w_in": nrm((n_b, d, 2 * GMLP_WIDTH), d ** -0.5),
        "gmlp_v_ln_g": 1.0 + nrm((n_b, GMLP_WIDTH), 0.02),
        "gmlp_v_ln_b": nrm((n_b, GMLP_WIDTH), 0.02),
        "gmlp_ws": nrm((n_b, GMLP_HEADS, GMLP_CHUNK, GMLP_CHUNK), GMLP_CHUNK ** -0.5),
        "gmlp_bs": 1.0 + nrm((n_b, GMLP_HEADS, GMLP_CHUNK), 0.1),
        "gmlp_w_out": nrm((n_b, GMLP_WIDTH, d), GMLP_WIDTH ** -0.5 * DEEPNORM_BETA),
        "conv_w_in": nrm((n_c, d, 2 * d), d ** -0.5),
        "conv_b_in": nrm((n_c, 2 * d), 0.02),
        "conv_dw": nrm((n_c, CONV_WIDTH, d), CONV_WIDTH ** -0.5),
        "conv_dw_b": nrm((n_c, d), 0.02),
        "conv_ln_g": 1.0 + nrm((n_c, d), 0.02),
        "conv_ln_b": nrm((n_c, d), 0.02),
        "conv_w_out": nrm((n_c, d, d), d ** -0.5 * DEEPNORM_BETA),
        "conv_b_out": nrm((n_c, d), 0.02),
        "sc_w_in": nrm((n_d, d, 3 * d), d ** -0.5),
        "sc_conv": nrm((n_d, SHORT_CONV_WIDTH, d), SHORT_CONV_WIDTH ** -0.5),
        "sc_w_out": nrm((n_d, d, d), d ** -0.5 * DEEPNORM_BETA),
    }


def reference(x, ln_g, ln_b, ffn_w_in, ffn_w_out, pool_w, pool_scale,
              gmlp_w_in, gmlp_v_ln_g, gmlp_v_ln_b, gmlp_ws, gmlp_bs, gmlp_w_out,
              conv_w_in, conv_b_in, conv_dw, conv_dw_b, conv_ln_g, conv_ln_b,
              conv_w_out, conv_b_out, sc_w_in, sc_conv, sc_w_out):
    alpha = DEEPNORM_ALPHA
    for i in range(DEPTH):
        m, j = i % N_MIXERS, i // N_MIXERS
        x = _layer_norm(alpha * x + FFN_RES_WEIGHT * _swiglu(x, ffn_w_in[i, 0], ffn_w_out[i, 0]),
                        ln_g[i, 0], ln_b[i, 0])
        if m == 0:
            h = _pool_mixer(x, pool_w[j], pool_scale[j])
        elif m == 1:
            h = _gmlp_mixer(x, gmlp_w_in[j], gmlp_v_ln_g[j], gmlp_v_ln_b[j],
                            gmlp_ws[j], gmlp_bs[j], gmlp_w_out[j])
        elif m == 2:
            h = _conformer_conv(x, conv_w_in[j], conv_b_in[j], conv_dw[j], conv_dw_b[j],
                                conv_ln_g[j], conv_ln_b[j], conv_w_out[j], conv_b_out[j])
        else:
            h = _short_conv(x, sc_w_in[j], sc_conv[j], sc_w_out[j])
        x = _layer_norm(alpha * x + h, ln_g[i, 1], ln_b[i, 1])
        x = _layer_norm(alpha * x + FFN_RES_WEIGHT * _swiglu(x, ffn_w_in[i, 1], ffn_w_out[i, 1]),
                        ln_g[i, 2], ln_b[i, 2])
    return x
```

```python
import os
import numpy as np
from contextlib import ExitStack
import concourse.bass as bass
import concourse.mybir as mybir
from concourse.bass_utils import run_bass_kernel_spmd

F32 = mybir.dt.float32
BF16 = mybir.dt.bfloat16
AF = mybir.ActivationFunctionType
ALU = mybir.AluOpType

D = 1024
KC = 8
T = 1024
TT = 512
NT = T // TT
DFF = 2816
ALPHA = float(8 ** 0.25)
EPS = 1e-5
N_CORES = 8
SEM_EPOCH = 24000
LN_CAST_ENG = 'dve'
PUMP = int(os.environ.get('PUMP', '1'))
LN_MUL_ENG = 'dve'

WSHAPES = [
    ("ln_g", [4, 3, 1024]), ("ln_b", [4, 3, 1024]),
    ("ffn_w_in", [4, 2, 1024, 5632]), ("ffn_w_out", [4, 2, 2816, 1024]),
    ("pool_w", [1, 4, 256, 256]), ("pool_scale", [1, 1024]),
    ("gmlp_w_in", [1, 1024, 4096]), ("gmlp_v_ln_g", [1, 2048]), ("gmlp_v_ln_b", [1, 2048]),
    ("gmlp_ws", [1, 8, 128, 128]), ("gmlp_bs", [1, 8, 128]), ("gmlp_w_out", [1, 2048, 1024]),
    ("conv_w_in", [1, 1024, 2048]), ("conv_b_in", [1, 2048]), ("conv_dw", [1, 31, 1024]),
    ("conv_dw_b", [1, 1024]), ("conv_ln_g", [1, 1024]), ("conv_ln_b", [1, 1024]),
    ("conv_w_out", [1, 1024, 1024]), ("conv_b_out", [1, 1024]),
    ("sc_w_in", [1, 1024, 3072]), ("sc_conv", [1, 3, 1024]), ("sc_w_out", [1, 1024, 1024]),
]


class Prog:
    def __init__(self, nc, es):
        self.nc = nc
        self.es = es
        self.engs = {"pe": nc.tensor, "act": nc.scalar, "dve": nc.vector, "pool": nc.gpsimd, "sp": nc.sync}
        self.q = {e: [] for e in self.engs}
        self.sems = {}
        self.cnt = {}
        self.epoch = {}
        self.seen = {e: {} for e in self.engs}
        self.lw = {}
        self.rd = {}
        self.fences = {}
        self.planning = False
        self.nbank = 0
        self.n_ops = 0
        self.bg = None
        self.bg_tile = None
        self._in_bg = False

    def _sem(self, base, amount):
        ep = self.epoch.get(base, 0)
        name = "%s_%d" % (base, ep)
        if name in self.sems and self.cnt[name] + amount > SEM_EPOCH:
            ep += 1
            self.epoch[base] = ep
            name = "%s_%d" % (base, ep)
        if name not in self.sems:
            self.sems[name] = self.es.enter_context(self.nc.semaphore(name))
            self.cnt[name] = 0
        self.cnt[name] += amount
        return name, self.cnt[name]

    def fence(self, prefix):
        if self.planning:
            return
        f = dict(self.fences.get(prefix, {}))
        for tab in (self.lw, self.rd):
            for k in [k for k in tab if isinstance(k[0], str) and k[0].startswith(prefix)]:
                for s, v in tab[k].items():
                    if f.get(s, 0) < v:
                        f[s] = v
                del tab[k]
        self.fences[prefix] = f

    def fence_merge(self, dst, src):
        if self.planning:
            return
        f = dict(self.fences.get(dst, {}))
        for sname, v in self.fences.get(src, {}).items():
            if f.get(sname, 0) < v:
                f[sname] = v
        self.fences[dst] = f

    def start_bg(self, gen, tile, eager=0):
        if self.planning:
            return
        self.drain()
        self.bg = gen
        self.bg_tile = tile
        self.pump(eager)

    def pump(self, k=1):
        if self.planning or self.bg is None or self._in_bg:
            return
        self._in_bg = True
        try:
            for _ in range(k):
                try:
                    next(self.bg)
                except StopIteration:
                    self.bg = None
                    break
        finally:
            self._in_bg = False

    def drain(self):
        self.pump(1 << 30)

    def bank(self):
        b = self.nbank % 6
        self.nbank += 1
        return b

    def op(self, eng, fn, reads=(), writes=(), dma_sem=None):
        if self.planning:
            return
        if self.bg is not None and not self._in_bg:
            for k in list(reads) + list(writes):
                if k[0] in ("xf", "xb", "XO:x") and k[-1] == self.bg_tile:
                    self.drain()
                    break
        self.n_ops += 1
        deps = {}

        def add(d):
            for s, v in d.items():
                if deps.get(s, 0) < v:
                    deps[s] = v

        for k in list(reads) + list(writes):
            if k in self.lw:
                add(self.lw[k])
            else:
                for pfx, f in self.fences.items():
                    if isinstance(k[0], str) and k[0].startswith(pfx):
                        add(f)
        for k in writes:
            if k in self.rd:
                add(self.rd[k])
        waits = []
        seen = self.seen[eng]
        for s, v in deps.items():
            if eng == "pe" and s.startswith("pe_"):
                continue
            if seen.get(s, 0) < v:
                seen[s] = v
                waits.append((s, v))
        if dma_sem is not None:
            sname, val = self._sem(dma_sem, 16)
            inc = 16
        else:
            sname, val = self._sem(eng, 1)
            inc = 1
        self.q[eng].append((waits, fn, sname, inc))
        h = {sname: val}
        for k in writes:
            self.lw[k] = h
            self.rd.pop(k, None)
        for k in reads:
            if k in writes:
                continue
            r = self.rd.setdefault(k, {})
            if r.get(sname, 0) < val:
                r[sname] = val
        return h

    def replay(self, eng, e):
        for waits, fn, sname, inc in self.q[eng]:
            for s, v in waits:
                e.wait_ge(self.sems[s], v)
            ins = fn(e)
            ins.then_inc(self.sems[sname], inc)

    def final_waits(self, eng, e, handles):
        for s, v in handles.items():
            e.wait_ge(self.sems[s], v)


class Ring:
    MAXP = 3

    def __init__(self, p, name, slots, queue="pool"):
        self.p = p
        self.name = name
        self.slots = slots
        self.n = len(slots)
        self.plan = []
        self.next_get = 0
        self.next_emit = 0
        self.last_done = -1
        self.queue = queue

    def keys(self, si):
        return [(self.name, si, q) for q in range(self.MAXP)]

    def _fill(self, upto):
        while self.next_emit <= upto and self.next_emit < len(self.plan):
            t = self.next_emit
            si = t % self.n
            h = None
            for q, (dst_fn, src) in enumerate(self.plan[t]):
                dst = dst_fn(self.slots[si])
                h = self.p.op(self.queue, (lambda e, d=dst, s=src: e.dma_start(out=d, in_=s)),
                              writes=[(self.name, si, q)],
                              dma_sem="%s%d" % (self.name, si))
            for k in self.keys(si):
                self.p.lw[k] = h
            self.next_emit += 1

    def get(self, loads):
        t = self.next_get
        self.next_get += 1
        if self.p.planning:
            assert len(loads) <= self.MAXP
            self.plan.append(loads)
            return self.slots[t % self.n], self.keys(t % self.n), t
        self._fill(min(t + self.n - 1, self.last_done + self.n))
        return self.slots[t % self.n], self.keys(t % self.n), t

    def done(self, t):
        if self.p.planning:
            return
        self.last_done = max(self.last_done, t)
        self._fill(self.last_done + self.n)

    def reset(self):
        self.next_get = 0
        self.next_emit = 0
        self.last_done = -1


def build_nc(NU, nlayers=4, sub_stop=None):
    nc = bass.Bass("TRN2", target_bir_lowering=False)
    x_d = nc.dram_tensor("x", [NU * T, D], F32, kind="ExternalInput").ap()
    out_d = nc.dram_tensor("out", [NU * T, D], F32, kind="ExternalOutput").ap()
    Wd = {name: nc.dram_tensor(name, shape, F32, kind="ExternalInput").ap() for name, shape in WSHAPES
          if not ('smallw' in os.environ.get('KDBG', '') and int(np.prod(shape)) > 300000)}

    es = ExitStack()
    with es:
        def sb(name, shape, dt):
            return es.enter_context(nc.sbuf_tensor(name, shape, dt))

        xf = sb("xf", [128, KC, T], F32)
        xb = sb("xb", [128, KC, T], BF16)
        arena_b = sb("arena_b", [128, 24576], BF16)
        arena_f = sb("arena_f", [128, 4224], F32)
        wa_t = sb("wa", [128, 4, 8, 512], BF16)
        wb_t = sb("wb", [128, 2, 4, 1024], BF16)
        ysq_t = sb("ysq", [128, 4, TT], BF16)
        yb_t = sb("yb", [128, 4, TT], BF16)
        lnm = sb("lnm", [128, 3, TT], F32)
        lnt = sb("lnt", [128, 4, TT], F32)
        stt_t = sb("sttmp", [128, 2, TT], F32)
        ident_f = sb("ident_f", [128, 128], F32)
        ident_b = sb("ident_b", [128, 128], BF16)
        ones_b = sb("ones_b", [128, 128], BF16)
        ones_f = sb("ones_f", [128, 128], F32)
        NCOL = 552
        cols = sb("cols", [128, NCOL], F32)
        acols = sb("acols", [128, 192], F32)
        wmt_b = sb("wmt_b", [128, 8, 128], BF16)
        cj = sb("cj", [128, 16, 128], F32)
        invcnt = sb("invcnt", [128, 4, 16], F32)
        poolw = sb("poolw", [128, 4, 2, 256], BF16)
        pool_halo = sb("pool_halo", [128, 8, 16], BF16)
        pool_mid = sb("pool_mid", [128, 8, 16], BF16)
        conv_halo = sb("conv_halo", [128, 8, 32], BF16)
        sc_halo = sb("sc_halo", [128, 8, 2], F32)
        gst = sb("gst", [128, 2, 40], F32)
        svt = sb("svt", [128, 2, 128], F32)
        small = sb("small", [128, 2, 16], F32)
        PS = [es.enter_context(nc.psum_tensor("ps%d" % i, [128, 512], F32)) for i in range(8)]

        wmt_f = arena_f[:, 1024:2048].rearrange("p (h t) -> p h t", h=8)
        bsbc = arena_f[:, 2048:3072].rearrange("p (h t) -> p h t", h=8)
        xo = arena_b[:, 8192:24576].bitcast(F32).rearrange("p (c t) -> p c t", c=8)
        p = Prog(nc, es)
        wa = Ring(p, "wa", [wa_t[:, i] for i in range(4)])
        wb = Ring(p, "wb", [wb_t[:, i] for i in range(2)])

        off = {}
        o = 0
        for nm, r in [("ln_g", 96), ("ln_b", 96), ("pool_scale", 8), ("conv_b_in", 16), ("conv_dw", 248),
                      ("conv_dw_b", 8), ("conv_ln_g", 8), ("conv_ln_b", 8), ("conv_b_out", 8), ("sc_conv", 24),
                      ("gmlp_v_ln_g", 16), ("gmlp_v_ln_b", 16)]:
            off[nm] = (o, r)
            o += r
        assert o == NCOL

        def col(nm, i):
            b = off[nm][0] + i
            return cols[:, b:b + 1]

        def mm(out_ap, pairs, reads, bankkey, extra_writes=()):
            def fn(e, out_ap=out_ap, pairs=pairs):
                n = len(pairs)
                ins = None
                for i, (l, r) in enumerate(pairs):
                    ins = e.matmul(out_ap, l, r, start=(i == 0), stop=(i == n - 1))
                return ins
            h = p.op("pe", fn, reads=reads, writes=[("ps", bankkey)] + list(extra_writes))
            p.pump(PUMP)
            return h

        def setup():
            p.op("pool", lambda e: e.memset(ident_f[:], 0.0), writes=[("ident_f",)])
            p.op("pool", lambda e: e.affine_select(out=ident_f[:], in_=ident_f[:], compare_op=ALU.not_equal, fill=1.0,
                                                   base=0, pattern=[[-1, 128]], channel_multiplier=1),
                 reads=[("ident_f",)], writes=[("ident_f",)])
            p.op("dve", lambda e: e.tensor_copy(out=ident_b[:], in_=ident_f[:]), reads=[("ident_f",)], writes=[("ident_b",)])
            p.op("dve", lambda e: e.memset(ones_b[:], 1.0 / 1024.0), writes=[("ones_b",)])
            p.op("dve", lambda e: e.memset(ones_f[:], 1.0), writes=[("ones_f",)])
            stage = arena_f[:, 0:256]
            si = 0
            for nm, (o0, r) in (off.items() if 'noparam' not in os.environ.get('KDBG', '') else []):
                ap = Wd[nm]
                names = "abcdefg"[:len(ap.shape) - 1]
                flat = ap.rearrange("%s (c p) -> (%s c) p" % (" ".join(names), " ".join(names)), p=128)
                for r0 in range(0, r, 128):
                    rr = min(128, r - r0)
                    st = stage[:, (si % 2) * 128:(si % 2) * 128 + 128]
                    key = ("AF:stage", si % 2)
                    p.op("sp", lambda e, st=st, rr=rr, src=flat[r0:r0 + rr, :]: e.dma_start(out=st[0:rr, :], in_=src),
                         writes=[key], dma_sem="stg%d" % (si % 2))
                    b = p.bank()
                    p.op("pe", lambda e, b=b, st=st, rr=rr: e.transpose(PS[b][:, 0:rr], st[0:rr, :], ident_f[0:rr, 0:rr]),
                         reads=[key, ("ident_f",)], writes=[("ps", b)])
                    p.op("act", lambda e, b=b, rr=rr, c0=o0 + r0: e.activation(out=cols[:, c0:c0 + rr], in_=PS[b][:, 0:rr], func=AF.Copy),
                         reads=[("ps", b)], writes=[("cols",)])
                    si += 1
            p.op("act", lambda e: e.mul(acols[:, 0:192], cols[:, 0:192], ALPHA), reads=[("cols",)], writes=[("acols",)])
            for g in (range(4) if 'nopool' not in os.environ.get('KDBG', '') else []):
                win = 2 << g
                p.op("dve", lambda e, g=g, win=win: e.memset(invcnt[:, g, :], 1.0 / win), writes=[("invcnt",)])
                for t in range(win - 1):
                    p.op("dve", lambda e, g=g, t=t: e.memset(invcnt[:, g, t:t + 1], 1.0 / (t + 1)), writes=[("invcnt",)])
                p.op("pool", lambda e, g=g: e.dma_start(out=poolw[:, g], in_=Wd["pool_w"][0, g].rearrange("(i p) o -> p i o", p=128)),
                     writes=[("poolw",)], dma_sem="misc")
            if 'nogmlp' in os.environ.get('KDBG', ''):
                return
            p.op("sp", lambda e: e.dma_start(out=bsbc[:], in_=Wd["gmlp_bs"][0].partition_broadcast(128)),
                 writes=[("AF:bsbc",)], dma_sem="misc2")
            for h in range(8):
                st = stage[:, (si % 2) * 128:(si % 2) * 128 + 128]
                key = ("AF:stage", si % 2)
                p.op("sp", lambda e, st=st, h=h: e.dma_start(out=st, in_=Wd["gmlp_ws"][0, h]), writes=[key], dma_sem="stg%d" % (si % 2))
                p.op("pool", lambda e, st=st: e.affine_select(out=st, in_=st, compare_op=ALU.is_ge, fill=0.0, base=0,
                                                              pattern=[[-1, 128]], channel_multiplier=1),
                     reads=[key], writes=[key])
                b = p.bank()
                p.op("pe", lambda e, b=b, st=st: e.transpose(PS[b][:, 0:128], st, ident_f[:]),
                     reads=[key, ("ident_f",)], writes=[("ps", b)])
                p.op("act", lambda e, b=b, h=h: e.activation(out=wmt_f[:, h, :], in_=PS[b][:, 0:128], func=AF.Copy),
                     reads=[("ps", b)], writes=[("AF:wmt_f", h)])
                p.op("act", lambda e, b=b, h=h: e.activation(out=wmt_b[:, h, :], in_=PS[b][:, 0:128], func=AF.Copy),
                     reads=[("ps", b)], writes=[("wmt_b",)])
                b2 = p.bank()
                p.op("pe", lambda e, b2=b2, h=h: e.matmul(PS[b2][:, 0:128], ones_f[:], wmt_f[:, h, :], start=True, stop=True),
                     reads=[("AF:wmt_f", h), ("ones_f",)], writes=[("ps", b2)])
                for j in (2 * h, 2 * h + 1):
                    p.op("dve", lambda e, b2=b2, h=h, j=j: e.scalar_tensor_tensor(
                        out=cj[:, j, :], in0=PS[b2][:, 0:128], scalar=col("gmlp_v_ln_b", j), in1=bsbc[:, h, :],
                        op0=ALU.mult, op1=ALU.add), reads=[("ps", b2), ("cols",), ("AF:bsbc",)], writes=[("cj",)])
                si += 1

        def ln_tile_gen(srcs, src_keys, apply_fn):
            LAG = 2
            for ci in range(8 + LAG):
                if ci < 8:
                    c, r = ci, ci % 4
                    p.op("act", lambda e, c=c, r=r: e.activation(out=ysq_t[:, r, :], in_=srcs[c], func=AF.Square),
                         reads=[src_keys[c]], writes=[("ysq", r)])
                    p.op("dve", lambda e, c=c, r=r: e.tensor_copy(out=yb_t[:, r, :], in_=srcs[c]),
                         reads=[src_keys[c]], writes=[("yb", r)])
                if ci >= LAG:
                    c, r = ci - LAG, (ci - LAG) % 4

                    def fn(e, c=c, r=r):
                        e.matmul(PS[6][:], ones_b[:], yb_t[:, r, :], start=(c == 0), stop=(c == 7))
                        return e.matmul(PS[7][:], ones_b[:], ysq_t[:, r, :], start=(c == 0), stop=(c == 7))
                    p.op("pe", fn, reads=[("yb", r), ("ysq", r), ("ones_b",)], writes=[("ps", 6), ("ps", 7)])
                yield
            mean, tmp, rstd = lnm[:, 0, :], lnm[:, 1, :], lnm[:, 2, :]
            p.op("act", lambda e: e.activation(out=mean, in_=PS[6][:], func=AF.Copy), reads=[("ps", 6)], writes=[("lnm", 0)])
            p.op("act", lambda e: e.activation(out=tmp, in_=PS[6][:], func=AF.Square), reads=[("ps", 6)], writes=[("lnm", 1)])
            p.op("dve", lambda e: e.scalar_tensor_tensor(out=tmp, in0=PS[7][:], scalar=EPS, in1=tmp, op0=ALU.add, op1=ALU.subtract),
                 reads=[("ps", 7), ("lnm", 1)], writes=[("lnm", 1)])
            p.op("act", lambda e: e.activation(out=tmp, in_=tmp, func=AF.Sqrt), reads=[("lnm", 1)], writes=[("lnm", 1)])
            p.op("dve", lambda e: e.reciprocal(out=rstd, in_=tmp), reads=[("lnm", 1)], writes=[("lnm", 2)])
            yield
            for c0 in range(0, 8, 2):
                tbs = [(c, lnt[:, c % 4, :], ("lnt", c % 4)) for c in (c0, c0 + 1)]
                for c, tb, tk in tbs:
                    p.op("dve", lambda e, c=c, tb=tb: e.tensor_tensor(out=tb, in0=srcs[c], in1=mean, op=ALU.subtract),
                         reads=[src_keys[c], ("lnm", 0)], writes=[tk])
                for c, tb, tk in tbs:
                    p.op(LN_MUL_ENG, lambda e, tb=tb: e.tensor_tensor(out=tb, in0=tb, in1=rstd, op=ALU.mult),
                         reads=[tk, ("lnm", 2)], writes=[tk])
                for c, tb, tk in tbs:
                    apply_fn(c, tb, tk)
                yield

        def ln_tile(srcs, src_keys, apply_fn):
            p.drain()
            for _ in ln_tile_gen(srcs, src_keys, apply_fn):
                pass


        def ln_main(l, s, n, last):
            li = l * 3 + s
            sl = slice(n * TT, (n + 1) * TT)
            srcs = [xf[:, c, sl] for c in range(8)]
            keys = [("xf", c, n) for c in range(8)]

            def apply(c, tb, tk):
                gi = li * 8 + c
                g_ap, b_ap = col("ln_g", gi), col("ln_b", gi)
                if last:
                    p.op("act", lambda e: e.activation(out=xo[:, c, sl], in_=tb, func=AF.Identity, bias=b_ap, scale=g_ap),
                         reads=[tk, ("cols",)], writes=[("XO:x", c, n)])
                    return
                p.op("act", lambda e: e.activation(out=xb[:, c, sl], in_=tb, func=AF.Identity, bias=b_ap, scale=g_ap),
                     reads=[tk, ("cols",)], writes=[("xb", c, n)])
                ag, ab = acols[:, gi:gi + 1], acols[:, 96 + gi:96 + gi + 1]
                p.op("act", lambda e: e.activation(out=xf[:, c, sl], in_=tb, func=AF.Identity, bias=ab, scale=ag),
                     reads=[tk, ("cols",), ("acols",)], writes=[("xf", c, n)])
            if p.planning:
                return
            p.start_bg(ln_tile_gen(srcs, keys, apply), n, eager=1)

        def outproj(wsrc, nk_total, rhs_fn, rhs_keys_fn, tiles, first_bias=None, scale=None, after_tile=None):
            blocks = [(k0, min(k0 + 4, nk_total)) for k0 in range(0, nk_total, 4)]
            for bi, (k0, k1) in enumerate(blocks):
                nk = k1 - k0
                slot, si, tk = wb.get([(lambda s, nk=nk: s[:, 0:nk, :], wsrc[:, k0:k1, :])])
                for n in tiles:
                    sl = slice(n * TT, (n + 1) * TT)
                    for c in range(8):
                        b = p.bank()
                        mm(PS[b][:], [(slot[:, kk, c * 128:(c + 1) * 128], rhs_fn(k0 + kk, n)) for kk in range(nk)],
                           reads=si + [rhs_keys_fn(k0 + kk, n) for kk in range(nk)], bankkey=b)
                        if scale is not None:
                            sc = scale(c) if callable(scale) else scale
                            rk = [("cols",)] if callable(scale) else []
                            p.op("dve", lambda e, b=b, c=c, sl=sl, sc=sc: e.scalar_tensor_tensor(
                                out=xf[:, c, sl], in0=PS[b][:], scalar=sc, in1=xf[:, c, sl], op0=ALU.mult, op1=ALU.add),
                                reads=[("ps", b), ("xf", c, n)] + rk, writes=[("xf", c, n)])
                        elif first_bias is not None and bi == 0:
                            p.op("dve", lambda e, b=b, c=c, sl=sl: e.scalar_tensor_tensor(
                                out=xf[:, c, sl], in0=PS[b][:], scalar=first_bias(c), in1=xf[:, c, sl], op0=ALU.add, op1=ALU.add),
                                reads=[("ps", b), ("xf", c, n), ("cols",)], writes=[("xf", c, n)])
                        else:
                            p.op("dve", lambda e, b=b, c=c, sl=sl: e.tensor_tensor(out=xf[:, c, sl], in0=PS[b][:], in1=xf[:, c, sl], op=ALU.add),
                                 reads=[("ps", b), ("xf", c, n)], writes=[("xf", c, n)])
                    if bi == len(blocks) - 1 and after_tile is not None:
                        after_tile(n)
                wb.done(tk)

        def ffn(l, s, last):
            win = Wd["ffn_w_in"][l, s].rearrange("(k p) f -> p k f", p=128)
            wout = Wd["ffn_w_out"][l, s].rearrange("(j p) d -> p j d", p=128)
            p.fence("AB:")
            if last:
                p.fence("XO:")
                p.fence_merge("XO:", "AB:")
            hid = arena_b[:, 0:6 * T].rearrange("p (j t) -> p j t", j=6)
            blocks = [(0, 2), (2, 6), (6, 10), (10, 14), (14, 16), (16, 18), (18, 22)]
            wbs = {}
            cnt = [0]

            def p1(j, j0, jj, slot, si, n):
                sl = slice(n * TT, (n + 1) * TT)
                bg, bu = p.bank(), p.bank()
                xk = [("xb", k, n) for k in range(8)]
                mm(PS[bg][:], [(slot[:, k, jj * 128:(jj + 1) * 128], xb[:, k, sl]) for k in range(8)],
                   reads=si + xk, bankkey=bg)
                mm(PS[bu][:], [(slot[:, k, 256 + jj * 128:256 + (jj + 1) * 128], xb[:, k, sl]) for k in range(8)],
                   reads=si + xk, bankkey=bu)
                st = stt_t[:, cnt[0] % 2, :]
                sk = ("sttmp", cnt[0] % 2)
                cnt[0] += 1
                p.op("act", lambda e: e.activation(out=st, in_=PS[bg][:], func=AF.Silu), reads=[("ps", bg)], writes=[sk])
                p.op("dve", lambda e: e.tensor_tensor(out=hid[:, j - j0, sl], in0=PS[bu][:], in1=st, op=ALU.mult),
                     reads=[("ps", bu), sk], writes=[("AB:hid", j - j0, n)])

            def p2(parts, n):
                sl = slice(n * TT, (n + 1) * TT)
                for c in range(8):
                    b = p.bank()
                    pairs, rk, h0 = [], [], 0
                    for slot, si, nj in parts:
                        pairs += [(slot[:, jl, c * 128:(c + 1) * 128], hid[:, h0 + jl, sl]) for jl in range(nj)]
                        rk += si + [("AB:hid", h0 + jl, n) for jl in range(nj)]
                        h0 += nj
                    mm(PS[b][:], pairs, reads=rk, bankkey=b)
                    p.op("dve", lambda e, b=b, c=c: e.scalar_tensor_tensor(
                        out=xf[:, c, sl], in0=PS[b][:], scalar=0.5, in1=xf[:, c, sl], op0=ALU.mult, op1=ALU.add),
                        reads=[("ps", b), ("xf", c, n)], writes=[("xf", c, n)])

            groups = [(0, 6, True), (6, 10, False), (10, 14, False), (14, 16, False), (16, 22, True)]
            for gi, (j0, j1, tile_outer) in enumerate(groups):
                is_last = gi == len(groups) - 1
                tks = []
                for jp in range(j0, j1, 2):
                    slot, si, tk = wa.get([(lambda s_: s_[:, :, 0:256], win[:, :, jp * 128:jp * 128 + 256]),
                                           (lambda s_: s_[:, :, 256:512], win[:, :, DFF + jp * 128:DFF + jp * 128 + 256])])
                    tks.append((jp, slot, si, tk))
                parts, wtk = [], []
                for k0 in range(j0, j1, 4):
                    k1 = min(k0 + 4, j1)
                    slotb, sib, tkb = wb.get([(lambda s_, nk=k1 - k0: s_[:, 0:nk, :], wout[:, k0:k1, :])])
                    parts.append((slotb, sib, k1 - k0))
                    wtk.append(tkb)
                if tile_outer:
                    for n in range(NT):
                        for jp, slot, si, tk in tks:
                            for jj in range(2):
                                p1(jp + jj, j0, jj, slot, si, n)
                        p2(parts, n)
                        if is_last:
                            ln_main(l, 2 * s, n, last)
                else:
                    for n in range(NT):
                        for jp, slot, si, tk in tks:
                            for jj in range(2):
                                p1(jp + jj, j0, jj, slot, si, n)
                    for n in range(NT):
                        p2(parts, n)
                for jp, slot, si, tk in tks:
                    wa.done(tk)
                for tkb in wtk:
                    wb.done(tkb)

        def mixer_pool(l, half):
            p.fence("AB:")
            p.fence("XO:")
            p.fence_merge("AB:", "XO:")
            p.fence("AF:")
            Pb = arena_b[:, 0:8 * T].rearrange("p (c t) -> p c t", c=8)
            W = 16 + TT
            for n in range(NT):
                sl = slice(n * TT, (n + 1) * TT)
                for g in range(4):
                    win = 2 << g
                    pair = []
                    for c in (2 * g, 2 * g + 1):
                        base = (c % 2) * 2 * W
                        pair.append((c, arena_f[:, base:base + W], arena_f[:, base + W:base + 2 * W], ("AF:pa", c % 2), ("AF:pb", c % 2)))
                    for c, A, B, ka, kb in pair:
                        if n == 0:
                            if half == 0:
                                p.op("dve", lambda e, A=A: e.memset(A[:, 0:16], 0.0), writes=[ka])
                            else:
                                p.op("dve", lambda e, A=A, c=c: e.tensor_copy(out=A[:, 0:16], in_=pool_halo[:, c, :]),
                                     reads=[("pool_halo", c)], writes=[ka])
                            p.op("dve", lambda e, c=c: e.tensor_copy(out=pool_mid[:, c, :], in_=xb[:, c, TT - 16:TT]),
                                 reads=[("xb", c, 0)], writes=[("pool_mid", c)])
                        else:
                            p.op("dve", lambda e, A=A, c=c: e.tensor_copy(out=A[:, 0:16], in_=pool_mid[:, c, :]),
                                 reads=[("pool_mid", c)], writes=[ka])
                    for c, A, B, ka, kb in pair:
                        p.op("act", lambda e, A=A, c=c, sl=sl: e.activation(out=A[:, 16:W], in_=xb[:, c, sl], func=AF.Copy),
                             reads=[("xb", c, n), ka], writes=[ka])
                        if half == 0 and n == NT - 1:
                            p.op("dve", lambda e, c=c: e.tensor_copy(out=pool_halo[:, c, :], in_=xb[:, c, T - 16:T]),
                                 reads=[("xb", c, n)], writes=[("pool_halo", c)])
                    valid = 0
                    flip = False
                    for m in range(g + 1):
                        sh = 1 << m
                        for c, A, B, ka, kb in pair:
                            src, dst, ks, kd = (B, A, kb, ka) if flip else (A, B, ka, kb)
                            p.op("dve", lambda e, src=src, dst=dst, sh=sh, v0=valid: e.tensor_tensor(
                                out=dst[:, v0 + sh:W], in0=src[:, v0 + sh:W], in1=src[:, v0:W - sh], op=ALU.add),
                                reads=[ks], writes=[kd])
                        valid += sh
                        flip = not flip
                    for c, A, B, ka, kb in pair:
                        src, ks = (B, kb) if flip else (A, ka)
                        p.op("dve", lambda e, src=src, c=c, sl=sl, win=win: e.scalar_tensor_tensor(
                            out=Pb[:, c, sl], in0=src[:, 16:W], scalar=1.0 / win, in1=xb[:, c, sl], op0=ALU.mult, op1=ALU.subtract),
                            reads=[ks, ("xb", c, n)], writes=[("AB:P", c, n)])
                    if half == 0 and n == 0:
                        for c, A, B, ka, kb in pair:
                            src, ks = (B, kb) if flip else (A, ka)
                            sm = small[:, c % 2, :]
                            p.op("dve", lambda e, src=src, sm=sm, g=g: e.tensor_tensor(out=sm, in0=src[:, 16:32], in1=invcnt[:, g, :], op=ALU.mult),
                                 reads=[ks, ("invcnt",)], writes=[("small", c % 2)])
                        for c, A, B, ka, kb in pair:
                            sm = small[:, c % 2, :]
                            p.op("dve", lambda e, c=c, sm=sm: e.tensor_tensor(out=Pb[:, c, 0:16], in0=sm, in1=xb[:, c, 0:16], op=ALU.subtract),
                                 reads=[("small", c % 2), ("xb", c, 0)], writes=[("AB:P", c, 0)])
                    p.pump(4)
                for g in range(4):
                    for oc in range(2):
                        c = 2 * g + oc
                        b = p.bank()
                        mm(PS[b][:], [(poolw[:, g, ic, oc * 128:(oc + 1) * 128], Pb[:, 2 * g + ic, sl]) for ic in range(2)],
                           reads=[("poolw",)] + [("AB:P", 2 * g + ic, n) for ic in range(2)], bankkey=b)
                        p.op("dve", lambda e, b=b, c=c, sl=sl: e.scalar_tensor_tensor(
                            out=xf[:, c, sl], in0=PS[b][:], scalar=col("pool_scale", c), in1=xf[:, c, sl], op0=ALU.mult, op1=ALU.add),
                            reads=[("ps", b), ("xf", c, n), ("cols",)], writes=[("xf", c, n)])
                ln_main(l, 1, n, False)

        def mixer_sc(l, half):
            p.fence("AB:")
            p.fence("XO:")
            p.fence_merge("AB:", "XO:")
            p.fence("AF:")
            G = arena_b[:, 0:8 * T].rearrange("p (c t) -> p c t", c=8)
            win = Wd["sc_w_in"][0].rearrange("(k p) f -> p k f", p=128)
            ub = arena_f[:, 0:1056]
            Bb = arena_f[:, 1056:2080]
            vb = arena_f[:, 2080:3104]
            cnt = 0
            for c in range(8):
                slot, si, tk = wa.get([(lambda s_, q=q: s_[:, :, q * 128:(q + 1) * 128], win[:, :, q * 1024 + c * 128:q * 1024 + (c + 1) * 128])
                                       for q in range(3)])
                if half == 0:
                    p.op("dve", lambda e: e.memset(ub[:, 0:32], 0.0), writes=[("AF:ubh",)])
                else:
                    p.op("dve", lambda e, c=c: e.tensor_copy(out=ub[:, 30:32], in_=sc_halo[:, c, :]),
                         reads=[("sc_halo", c)], writes=[("AF:ubh",)])
                for n in range(NT):
                    sl = slice(n * TT, (n + 1) * TT)
                    bB, bC, bh = p.bank(), p.bank(), p.bank()
                    xk = [("xb", k, n) for k in range(8)]
                    for q, b in ((0, bB), (1, bC), (2, bh)):
                        mm(PS[b][:], [(slot[:, k, q * 128:(q + 1) * 128], xb[:, k, sl]) for k in range(8)],
                           reads=si + xk, bankkey=b)
                    st = stt_t[:, cnt % 2, :]
                    sk = ("sttmp", cnt % 2)
                    cnt += 1
                    p.op("act", lambda e, bC=bC, st=st: e.activation(out=st, in_=PS[bC][:], func=AF.Copy), reads=[("ps", bC)], writes=[sk])
                    p.op("dve", lambda e, bh=bh, st=st, n=n: e.tensor_tensor(out=ub[:, 32 + n * TT:32 + (n + 1) * TT], in0=PS[bh][:], in1=st, op=ALU.mult),
                         reads=[("ps", bh), sk], writes=[("AF:ub", n)])
                    p.op("act", lambda e, bB=bB, sl=sl: e.activation(out=Bb[:, sl], in_=PS[bB][:], func=AF.Copy),
                         reads=[("ps", bB)], writes=[("AF:Bb", n)])
                wa.done(tk)
                ukeys = [("AF:ub", 0), ("AF:ub", 1), ("AF:ubh",)]
                p.op("dve", lambda e, c=c: e.tensor_scalar(out=vb, in0=ub[:, 32:1056], scalar1=col("sc_conv", 16 + c), scalar2=None, op0=ALU.mult),
                     reads=ukeys + [("cols",)], writes=[("AF:vb",)])
                p.op("dve", lambda e, c=c: e.scalar_tensor_tensor(out=vb, in0=ub[:, 31:1055], scalar=col("sc_conv", 8 + c), in1=vb, op0=ALU.mult, op1=ALU.add),
                     reads=ukeys + [("cols",), ("AF:vb",)], writes=[("AF:vb",)])
                p.op("dve", lambda e, c=c: e.scalar_tensor_tensor(out=vb, in0=ub[:, 30:1054], scalar=col("sc_conv", c), in1=vb, op0=ALU.mult, op1=ALU.add),
                     reads=ukeys + [("cols",), ("AF:vb",)], writes=[("AF:vb",)])
                p.op("dve", lambda e, c=c: e.tensor_tensor(out=G[:, c, :], in0=vb, in1=Bb, op=ALU.mult),
                     reads=[("AF:vb",), ("AF:Bb", 0), ("AF:Bb", 1)], writes=[("AB:G", c, 0), ("AB:G", c, 1)])
                if half == 0:
                    p.op("dve", lambda e, c=c: e.tensor_copy(out=sc_halo[:, c, :], in_=ub[:, 1054:1056]),
                         reads=[("AF:ub", 1)], writes=[("sc_halo", c)])
            wo = Wd["sc_w_out"][0].rearrange("(k p) d -> p k d", p=128)
            outproj(wo, 8, lambda k, n: G[:, k, n * TT:(n + 1) * TT], lambda k, n: ("AB:G", k, n), range(NT),
                    after_tile=lambda n: ln_main(l, 1, n, False))

        def mixer_conv(l, half):
            p.fence("AB:")
            p.fence("XO:")
            p.fence_merge("AB:", "XO:")
            p.fence("AF:")
            hg = arena_b[:, 0:8 * 1056].rearrange("p (c t) -> p c t", c=8)
            hn = arena_b[:, 8448:8448 + 8 * T].rearrange("p (c t) -> p c t", c=8)
            dg = arena_b[:, 16640:16640 + 2 * 31 * 128].rearrange("p (a k m) -> p a k m", a=2, k=31)
            hc = arena_f[:, 0:4096].rearrange("p (c t) -> p c t", c=8)
            win = Wd["conv_w_in"][0].rearrange("(k p) f -> p k f", p=128)
            cnt = 0
            tks = []
            for c2 in range(4):
                slot, si, tk = wa.get([(lambda s_: s_[:, :, 0:256], win[:, :, c2 * 256:(c2 + 1) * 256]),
                                       (lambda s_: s_[:, :, 256:512], win[:, :, 1024 + c2 * 256:1024 + (c2 + 1) * 256])])
                tks.append((slot, si, tk))
            for n in range(NT):
                sl = slice(n * TT, (n + 1) * TT)
                for c2 in range(4):
                    slot, si, tk = tks[c2]
                    for jj in range(2):
                        c = 2 * c2 + jj
                        if n == 0:
                            if half == 0:
                                p.op("dve", lambda e, c=c: e.memset(hg[:, c, 0:32], 0.0), writes=[("AB:hgh", c)])
                            else:
                                p.op("dve", lambda e, c=c: e.tensor_copy(out=hg[:, c, 0:32], in_=conv_halo[:, c, :]),
                                     reads=[("conv_halo", c)], writes=[("AB:hgh", c)])
                        ba, bg = p.bank(), p.bank()
                        xk = [("xb", k, n) for k in range(8)]
                        mm(PS[ba][:], [(slot[:, k, jj * 128:(jj + 1) * 128], xb[:, k, sl]) for k in range(8)], reads=si + xk, bankkey=ba)
                        mm(PS[bg][:], [(slot[:, k, 256 + jj * 128:256 + (jj + 1) * 128], xb[:, k, sl]) for k in range(8)], reads=si + xk, bankkey=bg)
                        st = stt_t[:, cnt % 2, :]
                        sk = ("sttmp", cnt % 2)
                        cnt += 1
                        p.op("act", lambda e, bg=bg, st=st, c=c: e.activation(out=st, in_=PS[bg][:], func=AF.Sigmoid, bias=col("conv_b_in", 8 + c)),
                             reads=[("ps", bg), ("cols",)], writes=[sk])
                        p.op("dve", lambda e, ba=ba, st=st, c=c, n=n: e.scalar_tensor_tensor(
                            out=hg[:, c, 32 + n * TT:32 + (n + 1) * TT], in0=PS[ba][:], scalar=col("conv_b_in", c), in1=st, op0=ALU.add, op1=ALU.mult),
                            reads=[("ps", ba), sk, ("cols",)], writes=[("AB:hg", c, n)])
                        if half == 0 and n == NT - 1:
                            p.op("dve", lambda e, c=c: e.tensor_copy(out=conv_halo[:, c, :], in_=hg[:, c, T:T + 32]),
                                 reads=[("AB:hg", c, 1)], writes=[("conv_halo", c)])
            for slot, si, tk in tks:
                wa.done(tk)
            dwo = off["conv_dw"][0]
            di = 0
            for n in range(NT):
                for c in range(8):
                    a = di % 2
                    di += 1
                    p.op("dve", lambda e, a=a, c=c: e.tensor_tensor(
                        out=dg[:, a], in0=ident_b[:].unsqueeze(1).broadcast_to([128, 31, 128]),
                        in1=cols[:, dwo + c:dwo + 248:8].unsqueeze(2).broadcast_to([128, 31, 128]), op=ALU.mult),
                        reads=[("ident_b",), ("cols",)], writes=[("AB:dg", a)])
                    b = p.bank()
                    rk = [("AB:dg", a), ("AB:hg", c, n)] + ([("AB:hg", c, n - 1)] if n > 0 else [("AB:hgh", c)])
                    mm(PS[b][:], [(dg[:, a, k, :], hg[:, c, 2 + k + n * TT:2 + k + (n + 1) * TT]) for k in range(31)], reads=rk, bankkey=b)
                    p.op("act", lambda e, b=b, c=c: e.activation(out=hc[:, c, :], in_=PS[b][:], func=AF.Identity, bias=col("conv_dw_b", c)),
                         reads=[("ps", b), ("cols",)], writes=[("AF:hc", c)])

                def apply(c, tb, tk, n=n):
                    p.op("act", lambda e: e.activation(out=hn[:, c, n * TT:(n + 1) * TT], in_=tb, func=AF.Silu,
                                                       bias=col("conv_ln_b", c), scale=col("conv_ln_g", c)),
                         reads=[tk, ("cols",)], writes=[("AB:hn", c, n)])
                ln_tile([hc[:, c, :] for c in range(8)], [("AF:hc", c) for c in range(8)], apply)
            wo = Wd["conv_w_out"][0].rearrange("(k p) d -> p k d", p=128)
            outproj(wo, 8, lambda k, n: hn[:, k, n * TT:(n + 1) * TT], lambda k, n: ("AB:hn", k, n), range(NT),
                    first_bias=lambda c: col("conv_b_out", c), after_tile=lambda n: ln_main(l, 1, n, False))

        def mixer_gmlp(l, half):
            p.fence("AB:")
            p.fence("XO:")
            p.fence_merge("AB:", "XO:")
            p.fence("AF:")
            wv = arena_b[:, 0:16384].rearrange("p (k e) -> p k e", k=8)
            u = arena_b[:, 16384:24576].rearrange("p (j t) -> p j t", j=16)
            vtm2 = arena_f[:, 0:4096].rearrange("p (a e) -> p a e", a=2)
            vn1 = ysq_t[:, :, :].rearrange("p r t -> p (r t)")
            win = Wd["gmlp_w_in"][0].rearrange("(k p) f -> p k f", p=128)
            for q in range(4):
                p.op("pool", lambda e, q=q: e.dma_start(out=wv[:, :, q * 512:(q + 1) * 512], in_=win[:, :, 2048 + q * 512:2048 + (q + 1) * 512]),
                     writes=[("AB:wv", q)], dma_sem="wv%d" % q)
            wo = Wd["gmlp_w_out"][0].rearrange("(k p) d -> p k d", p=128)
            vi = 0
            for n in range(NT):
                sl = slice(n * TT, (n + 1) * TT)
                xk = [("xb", k, n) for k in range(8)]
                for e4 in range(4):
                    slot, si, tk = wa.get([(lambda s_: s_[:, :, :], win[:, :, e4 * 512:(e4 + 1) * 512])])
                    for jj in range(4):
                        j = e4 * 4 + jj
                        b = p.bank()
                        mm(PS[b][:], [(slot[:, k, jj * 128:(jj + 1) * 128], xb[:, k, sl]) for k in range(8)], reads=si + xk, bankkey=b)
                        p.op("act", lambda e, b=b, j=j: e.activation(out=u[:, j, :], in_=PS[b][:], func=AF.Gelu_apprx_tanh),
                             reads=[("ps", b)], writes=[("AB:u", j)])
                    wa.done(tk)
                p.drain()
                vnk = [("ysq", r) for r in range(4)]

                def v_stage(tb):
                    t0 = n * TT + tb * 128
                    a = tb % 2
                    vt = vtm2[:, a, :]
                    bv = [p.bank() for _ in range(4)]
                    stats = gst[:, a, 0:24]
                    mv = gst[:, a, 24:26]
                    sd = gst[:, a, 26:27]
                    rstd = gst[:, a, 27:28]
                    nmr = gst[:, a, 28:29]
                    for et in range(4):
                        mm(PS[bv[et]][:], [(xb[:, k, t0:t0 + 128], wv[:, k, et * 512:(et + 1) * 512]) for k in range(8)],
                           reads=xk + [("AB:wv", et)], bankkey=bv[et])
                        p.op("act", lambda e, et=et, b=bv[et]: e.activation(out=vt[:, et * 512:(et + 1) * 512], in_=PS[b][:], func=AF.Gelu_apprx_tanh),
                             reads=[("ps", bv[et])], writes=[("AF:vtm", a, et)])
                        p.op("dve", lambda e, et=et: e.bn_stats(out=stats[:, et * 6:(et + 1) * 6], in_=vt[:, et * 512:(et + 1) * 512]),
                             reads=[("AF:vtm", a, et)], writes=[("gst", a, et)])
                    p.op("dve", lambda e: e.bn_aggr(out=mv, in_=stats),
                         reads=[("gst", a, et) for et in range(4)], writes=[("gmv", a)])
                    p.op("dve", lambda e: e.tensor_scalar(out=sd, in0=mv[:, 1:2], scalar1=EPS, scalar2=None, op0=ALU.add),
                         reads=[("gmv", a)], writes=[("gsd", a)])
                    p.op("act", lambda e: e.activation(out=sd, in_=sd, func=AF.Sqrt), reads=[("gsd", a)], writes=[("gsd", a)])
                    p.op("dve", lambda e: e.reciprocal(out=rstd, in_=sd), reads=[("gsd", a)], writes=[("grs", a)])
                    p.op("dve", lambda e: e.tensor_scalar(out=nmr, in0=mv[:, 0:1], scalar1=-1.0, scalar2=rstd, op0=ALU.mult, op1=ALU.mult),
                         reads=[("gmv", a), ("grs", a)], writes=[("gnm", a)])

                def norm(tb):
                    a = tb % 2
                    p.op("act", lambda e: e.activation(out=vn1, in_=vtm2[:, a, :], func=AF.Identity, bias=gst[:, a, 28:29], scale=gst[:, a, 27:28]),
                         reads=[("AF:vtm", a, et) for et in range(4)] + [("grs", a), ("gnm", a)], writes=vnk)

                def s_stage(tb):
                    si2 = 0
                    for j4 in range(4):
                        b = p.bank()

                        def fn(e, b=b, j4=j4):
                            ins = None
                            for q in range(4):
                                j = j4 * 4 + q
                                ins = e.matmul(PS[b][:, q * 128:(q + 1) * 128], vn1[:, j * 128:(j + 1) * 128], wmt_b[:, j // 2, :], start=True, stop=True)
                            return ins
                        p.op("pe", fn, reads=vnk + [("wmt_b",)], writes=[("ps", b)])
                        for q in range(4):
                            j = j4 * 4 + q
                            sv = svt[:, si2 % 2, :]
                            svk = ("svt", si2 % 2)
                            si2 += 1
                            p.op("dve", lambda e, b=b, q=q, j=j, sv=sv: e.scalar_tensor_tensor(
                                out=sv, in0=PS[b][:, q * 128:(q + 1) * 128], scalar=col("gmlp_v_ln_g", j), in1=cj[:, j, :], op0=ALU.mult, op1=ALU.add),
                                reads=[("ps", b), ("cols",), ("cj",)], writes=[svk])
                            p.op("dve", lambda e, j=j, sv=sv: e.tensor_tensor(
                                out=u[:, j, tb * 128:(tb + 1) * 128], in0=sv, in1=u[:, j, tb * 128:(tb + 1) * 128], op=ALU.mult),
                                reads=[svk, ("AB:u", j)], writes=[("AB:u", j)])

                v_stage(0)
                norm(0)
                for tb in range(4):
                    if tb + 1 < 4:
                        v_stage(tb + 1)
                    s_stage(tb)
                    if tb + 1 < 4:
                        norm(tb + 1)
                outproj(wo, 16, lambda k, n_: u[:, k, :], lambda k, n_: ("AB:u", k), [n],
                        after_tile=lambda n_: ln_main(l, 1, n_, False))

        NTB = T // 128
        ld_issued = set()

        def ld_stage(tb):
            return arena_f[:, 2048 + (tb % 2) * 1024:2048 + (tb % 2 + 1) * 1024], ("AF:ld", tb % 2)

        def load_dma(uidx, tb):
            if (uidx, tb) in ld_issued or uidx >= NU or tb >= NTB:
                return
            ld_issued.add((uidx, tb))
            st, key = ld_stage(tb)
            r0 = uidx * T + tb * 128
            p.op("sp", lambda e: e.dma_start(out=st, in_=x_d[r0:r0 + 128, :]), writes=[key], dma_sem="io%d" % (tb % 2))

        def load_tb(uidx, tb):
            load_dma(uidx, tb)
            st, key = ld_stage(tb)
            n = tb // 4
            for cg in range(2):
                b = p.bank()

                def fn(e, b=b, cg=cg):
                    ins = None
                    for q in range(4):
                        cc = cg * 4 + q
                        ins = e.transpose(PS[b][:, q * 128:(q + 1) * 128], st[:, cc * 128:(cc + 1) * 128], ident_f[:])
                    return ins
                p.op("pe", fn, reads=[key, ("ident_f",)], writes=[("ps", b)])
                cs = slice(tb * 128, (tb + 1) * 128)
                if cg == 0:
                    for q in range(4):
                        p.op("act", lambda e, b=b, q=q: e.activation(out=xf[:, q, cs], in_=PS[b][:, q * 128:(q + 1) * 128], func=AF.Copy, scale=ALPHA),
                             reads=[("ps", b)], writes=[("xf", q, n)])
                        p.op("act", lambda e, b=b, q=q: e.activation(out=xb[:, q, cs], in_=PS[b][:, q * 128:(q + 1) * 128], func=AF.Copy),
                             reads=[("ps", b)], writes=[("xb", q, n)])
                else:
                    psv = PS[b][:].rearrange("p (q t) -> p q t", q=4)
                    p.op("dve", lambda e, psv=psv: e.tensor_scalar(out=xf[:, 4:8, cs], in0=psv, scalar1=ALPHA, scalar2=None, op0=ALU.mult),
                         reads=[("ps", b)], writes=[("xf", 4 + q, n) for q in range(4)])
                    p.op("dve", lambda e, psv=psv: e.tensor_copy(out=xb[:, 4:8, cs], in_=psv),
                         reads=[("ps", b)], writes=[("xb", 4 + q, n) for q in range(4)])
            load_dma(uidx, tb + 2)

        out_handles = {}

        def store_tb(uidx, tb):
            st = arena_f[:, (tb % 2) * 1024:(tb % 2 + 1) * 1024]
            key = ("AF:st", tb % 2)
            n = tb // 4
            for cg in range(2):
                b = p.bank()

                def fn(e, b=b, cg=cg):
                    ins = None
                    for q in range(4):
                        cc = cg * 4 + q
                        ins = e.transpose(PS[b][:, q * 128:(q + 1) * 128], xo[:, cc, tb * 128:(tb + 1) * 128], ident_f[:])
                    return ins
                p.op("pe", fn, reads=[("XO:x", cg * 4 + q, n) for q in range(4)] + [("ident_f",)], writes=[("ps", b)])
                if cg == 0:
                    p.op("act", lambda e, b=b: e.activation(out=st[:, 0:512], in_=PS[b][:], func=AF.Copy), reads=[("ps", b)], writes=[key])
                else:
                    p.op("dve", lambda e, b=b: e.tensor_copy(out=st[:, 512:1024], in_=PS[b][:]), reads=[("ps", b)], writes=[key])
            r0 = uidx * T + tb * 128
            h = p.op("sp", lambda e: e.dma_start(out=out_d[r0:r0 + 128, :], in_=st), reads=[key], dma_sem="oo%d" % (tb % 2))
            if h:
                out_handles.update(h)

        mixers = [mixer_pool, mixer_gmlp, mixer_conv, mixer_sc]

        def program():
            ld_issued.clear()
            setup()
            p.fence("AF:")
            for tb in range(NTB):
                load_tb(0, tb)
            for uidx in range(NU):
                half = uidx % 2
                for l in range(nlayers):
                    last = (l == nlayers - 1)
                    nsub = 3 if sub_stop is None or l < nlayers - 1 else sub_stop
                    if nsub >= 1:
                        ffn(l, 0, False)
                    if nsub >= 2:
                        mixers[l](l, half)
                    if nsub >= 3:
                        if last:
                            p.fence("AF:")
                            load_dma(uidx + 1, 0)
                            load_dma(uidx + 1, 1)
                        ffn(l, 1, last)
                if nlayers == 0 or sub_stop not in (None, 3):
                    raise NotImplementedError("final LN -> xo is required (run with full last layer)")
                for tb in range(NTB):
                    store_tb(uidx, tb)
                    if uidx + 1 < NU:
                        load_tb(uidx + 1, tb)
                    p.pump(3)
            p.drain()

        p.planning = True
        program()
        p.planning = False
        wa.reset()
        wb.reset()
        p.nbank = 0
        program()

        block = es.enter_context(nc.Block())

        @block.tensor
        def _(e):
            p.replay("pe", e)

        @block.scalar
        def _(e):
            p.replay("act", e)

        @block.vector
        def _(e):
            p.replay("dve", e)

        @block.gpsimd
        def _(e):
            p.replay("pool", e)

        @block.sync
        def _(e):
            p.replay("sp", e)
            p.final_waits("sp", e, out_handles)
    build_nc.last_stats = {e: len(q) for e, q in p.q.items()}
    return nc


def kernel(**inputs):
    x = np.ascontiguousarray(inputs["x"], dtype=np.float32)
    B, S, _ = x.shape
    per = B // N_CORES
    NU = per * S // T
    nc = build_nc(NU)
    weights = {name: np.ascontiguousarray(inputs[name], dtype=np.float32) for name, _ in WSHAPES}
    in_maps = []
    for i in range(N_CORES):
        m = {"x": x[i * per:(i + 1) * per].reshape(per * S, D)}
        m.update(weights)
        in_maps.append(m)
    res = run_bass_kernel_spmd(nc, in_maps, core_ids=list(range(N_CORES)))
    outs = [np.asarray(r["out"]).reshape(per, S, D) for r in res.results]
    return np.concatenate(outs, axis=0).astype(np.float32, copy=False)
```

```python
import os
import numpy as np
from contextlib import ExitStack
import concourse.bass as bass
import concourse.mybir as mybir
from concourse.bass_utils import run_bass_kernel_spmd

F32 = mybir.dt.float32
BF16 = mybir.dt.bfloat16
AF = mybir.ActivationFunctionType
ALU = mybir.AluOpType

D = 1024
KC = 8
T = 1024
TT = 512
NT = T // TT
DFF = 2816
ALPHA = float(8 ** 0.25)
EPS = 1e-5
N_CORES = 8
SEM_EPOCH = 24000
LN_CAST_ENG = 'dve'
PUMP = int(os.environ.get('PUMP', '1'))
LN_MUL_ENG = 'dve'

WSHAPES = [
    ("ln_g", [4, 3, 1024]), ("ln_b", [4, 3, 1024]),
    ("ffn_w_in", [4, 2, 1024, 5632]), ("ffn_w_out", [4, 2, 2816, 1024]),
    ("pool_w", [1, 4, 256, 256]), ("pool_scale", [1, 1024]),
    ("gmlp_w_in", [1, 1024, 4096]), ("gmlp_v_ln_g", [1, 2048]), ("gmlp_v_ln_b", [1, 2048]),
    ("gmlp_ws", [1, 8, 128, 128]), ("gmlp_bs", [1, 8, 128]), ("gmlp_w_out", [1, 2048, 1024]),
    ("conv_w_in", [1, 1024, 2048]), ("conv_b_in", [1, 2048]), ("conv_dw", [1, 31, 1024]),
    ("conv_dw_b", [1, 1024]), ("conv_ln_g", [1, 1024]), ("conv_ln_b", [1, 1024]),
    ("conv_w_out", [1, 1024, 1024]), ("conv_b_out", [1, 1024]),
    ("sc_w_in", [1, 1024, 3072]), ("sc_conv", [1, 3, 1024]), ("sc_w_out", [1, 1024, 1024]),
]


class Prog:
    def __init__(self, nc, es):
        self.nc = nc
        self.es = es
        self.engs = {"pe": nc.tensor, "act": nc.scalar, "dve": nc.vector, "pool": nc.gpsimd, "sp": nc.sync}
        self.q = {e: [] for e in self.engs}
        self.sems = {}
        self.cnt = {}
        self.epoch = {}
        self.seen = {e: {} for e in self.engs}
        self.lw = {}
        self.rd = {}
        self.fences = {}
        self.planning = False
        self.nbank = 0
        self.n_ops = 0
        self.bg = None
        self.bg_tile = None
        self._in_bg = False

    def _sem(self, base, amount):
        ep = self.epoch.get(base, 0)
        name = "%s_%d" % (base, ep)
        if name in self.sems and self.cnt[name] + amount > SEM_EPOCH:
            ep += 1
            self.epoch[base] = ep
            name = "%s_%d" % (base, ep)
        if name not in self.sems:
            self.sems[name] = self.es.enter_context(self.nc.semaphore(name))
            self.cnt[name] = 0
        self.cnt[name] += amount
        return name, self.cnt[name]

    def fence(self, prefix):
        if self.planning:
            return
        f = dict(self.fences.get(prefix, {}))
        for tab in (self.lw, self.rd):
            for k in [k for k in tab if isinstance(k[0], str) and k[0].startswith(prefix)]:
                for s, v in tab[k].items():
                    if f.get(s, 0) < v:
                        f[s] = v
                del tab[k]
        self.fences[prefix] = f

    def fence_merge(self, dst, src):
        if self.planning:
            return
        f = dict(self.fences.get(dst, {}))
        for sname, v in self.fences.get(src, {}).items():
            if f.get(sname, 0) < v:
                f[sname] = v
        self.fences[dst] = f

    def start_bg(self, gen, tile, eager=0):
        if self.planning:
            return
        self.drain()
        self.bg = gen
        self.bg_tile = tile
        self.pump(eager)

    def pump(self, k=1):
        if self.planning or self.bg is None or self._in_bg:
            return
        self._in_bg = True
        try:
            for _ in range(k):
                try:
                    next(self.bg)
                except StopIteration:
                    self.bg = None
                    break
        finally:
            self._in_bg = False

    def drain(self):
        self.pump(1 << 30)

    def bank(self):
        b = self.nbank % 6
        self.nbank += 1
        return b

    def op(self, eng, fn, reads=(), writes=(), dma_sem=None):
        if self.planning:
            return
        if self.bg is not None and not self._in_bg:
            for k in list(reads) + list(writes):
                if k[0] in ("xf", "xb", "XO:x") and k[-1] == self.bg_tile:
                    self.drain()
                    break
        self.n_ops += 1
        deps = {}

        def add(d):
            for s, v in d.items():
                if deps.get(s, 0) < v:
                    deps[s] = v

        for k in list(reads) + list(writes):
            if k in self.lw:
                add(self.lw[k])
            else:
                for pfx, f in self.fences.items():
                    if isinstance(k[0], str) and k[0].startswith(pfx):
                        add(f)
        for k in writes:
            if k in self.rd:
                add(self.rd[k])
        waits = []
        seen = self.seen[eng]
        for s, v in deps.items():
            if eng == "pe" and s.startswith("pe_"):
                continue
            if seen.get(s, 0) < v:
                seen[s] = v
                waits.append((s, v))
        if dma_sem is not None:
            sname, val = self._sem(dma_sem, 16)
            inc = 16
        else:
            sname, val = self._sem(eng, 1)
            inc = 1
        self.q[eng].append((waits, fn, sname, inc))
        h = {sname: val}
        for k in writes:
            self.lw[k] = h
            self.rd.pop(k, None)
        for k in reads:
            if k in writes:
                continue
            r = self.rd.setdefault(k, {})
            if r.get(sname, 0) < val:
                r[sname] = val
        return h

    def replay(self, eng, e):
        for waits, fn, sname, inc in self.q[eng]:
            for s, v in waits:
                e.wait_ge(self.sems[s], v)
            ins = fn(e)
            ins.then_inc(self.sems[sname], inc)

    def final_waits(self, eng, e, handles):
        for s, v in handles.items():
            e.wait_ge(self.sems[s], v)


class Ring:
    MAXP = 3

    def __init__(self, p, name, slots, queue="pool"):
        self.p = p
        self.name = name
        self.slots = slots
        self.n = len(slots)
        self.plan = []
        self.next_get = 0
        self.next_emit = 0
        self.last_done = -1
        self.queue = queue

    def keys(self, si):
        return [(self.name, si, q) for q in range(self.MAXP)]

    def _fill(self, upto):
        while self.next_emit <= upto and self.next_emit < len(self.plan):
            t = self.next_emit
            si = t % self.n
            h = None
            for q, (dst_fn, src) in enumerate(self.plan[t]):
                dst = dst_fn(self.slots[si])
                h = self.p.op(self.queue, (lambda e, d=dst, s=src: e.dma_start(out=d, in_=s)),
                              writes=[(self.name, si, q)],
                              dma_sem="%s%d" % (self.name, si))
            for k in self.keys(si):
                self.p.lw[k] = h
            self.next_emit += 1

    def get(self, loads):
        t = self.next_get
        self.next_get += 1
        if self.p.planning:
            assert len(loads) <= self.MAXP
            self.plan.append(loads)
            return self.slots[t % self.n], self.keys(t % self.n), t
        self._fill(min(t + self.n - 1, self.last_done + self.n))
        return self.slots[t % self.n], self.keys(t % self.n), t

    def done(self, t):
        if self.p.planning:
            return
        self.last_done = max(self.last_done, t)
        self._fill(self.last_done + self.n)

    def reset(self):
        self.next_get = 0
        self.next_emit = 0
        self.last_done = -1


def build_nc(NU, nlayers=4, sub_stop=None):
    nc = bass.Bass("TRN2", target_bir_lowering=False)
    x_d = nc.dram_tensor("x", [NU * T, D], F32, kind="ExternalInput").ap()
    out_d = nc.dram_tensor("out", [NU * T, D], F32, kind="ExternalOutput").ap()
    Wd = {name: nc.dram_tensor(name, shape, F32, kind="ExternalInput").ap() for name, shape in WSHAPES
          if not ('smallw' in os.environ.get('KDBG', '') and int(np.prod(shape)) > 300000)}

    es = ExitStack()
    with es:
        def sb(name, shape, dt):
            return es.enter_context(nc.sbuf_tensor(name, shape, dt))

        xf = sb("xf", [128, KC, T], F32)
        xb = sb("xb", [128, KC, T], BF16)
        arena_b = sb("arena_b", [128, 24576], BF16)
        arena_f = sb("arena_f", [128, 4224], F32)
        wa_t = sb("wa", [128, 4, 8, 512], BF16)
        wb_t = sb("wb", [128, 2, 4, 1024], BF16)
        ysq_t = sb("ysq", [128, 4, TT], BF16)
        yb_t = sb("yb", [128, 4, TT], BF16)
        lnm = sb("lnm", [128, 3, TT], F32)
        lnt = sb("lnt", [128, 4, TT], F32)
        stt_t = sb("sttmp", [128, 2, TT], F32)
        ident_f = sb("ident_f", [128, 128], F32)
        ident_b = sb("ident_b", [128, 128], BF16)
        ones_b = sb("ones_b", [128, 128], BF16)
        ones_f = sb("ones_f", [128, 128], F32)
        NCOL = 552
        cols = sb("cols", [128, NCOL], F32)
        acols = sb("acols", [128, 192], F32)
        wmt_b = sb("wmt_b", [128, 8, 128], BF16)
        cj = sb("cj", [128, 16, 128], F32)
        invcnt = sb("invcnt", [128, 4, 16], F32)
        poolw = sb("poolw", [128, 4, 2, 256], BF16)
        pool_halo = sb("pool_halo", [128, 8, 16], BF16)
        pool_mid = sb("pool_mid", [128, 8, 16], BF16)
        conv_halo = sb("conv_halo", [128, 8, 32], BF16)
        sc_halo = sb("sc_halo", [128, 8, 2], F32)
        gst = sb("gst", [128, 2, 40], F32)
        svt = sb("svt", [128, 2, 128], F32)
        small = sb("small", [128, 2, 16], F32)
        PS = [es.enter_context(nc.psum_tensor("ps%d" % i, [128, 512], F32)) for i in range(8)]

        wmt_f = arena_f[:, 1024:2048].rearrange("p (h t) -> p h t", h=8)
        bsbc = arena_f[:, 2048:3072].rearrange("p (h t) -> p h t", h=8)
        xo = arena_b[:, 8192:24576].bitcast(F32).rearrange("p (c t) -> p c t", c=8)
        p = Prog(nc, es)
        wa = Ring(p, "wa", [wa_t[:, i] for i in range(4)])
        wb = Ring(p, "wb", [wb_t[:, i] for i in range(2)])

        off = {}
        o = 0
        for nm, r in [("ln_g", 96), ("ln_b", 96), ("pool_scale", 8), ("conv_b_in", 16), ("conv_dw", 248),
                      ("conv_dw_b", 8), ("conv_ln_g", 8), ("conv_ln_b", 8), ("conv_b_out", 8), ("sc_conv", 24),
                      ("gmlp_v_ln_g", 16), ("gmlp_v_ln_b", 16)]:
            off[nm] = (o, r)
            o += r
        assert o == NCOL

        def col(nm, i):
            b = off[nm][0] + i
            return cols[:, b:b + 1]

        def mm(out_ap, pairs, reads, bankkey, extra_writes=()):
            def fn(e, out_ap=out_ap, pairs=pairs):
                n = len(pairs)
                ins = None
                for i, (l, r) in enumerate(pairs):
                    ins = e.matmul(out_ap, l, r, start=(i == 0), stop=(i == n - 1))
                return ins
            h = p.op("pe", fn, reads=reads, writes=[("ps", bankkey)] + list(extra_writes))
            p.pump(PUMP)
            return h

        def setup():
            p.op("pool", lambda e: e.memset(ident_f[:], 0.0), writes=[("ident_f",)])
            p.op("pool", lambda e: e.affine_select(out=ident_f[:], in_=ident_f[:], compare_op=ALU.not_equal, fill=1.0,
                                                   base=0, pattern=[[-1, 128]], channel_multiplier=1),
                 reads=[("ident_f",)], writes=[("ident_f",)])
            p.op("dve", lambda e: e.tensor_copy(out=ident_b[:], in_=ident_f[:]), reads=[("ident_f",)], writes=[("ident_b",)])
            p.op("dve", lambda e: e.memset(ones_b[:], 1.0 / 1024.0), writes=[("ones_b",)])
            p.op("dve", lambda e: e.memset(ones_f[:], 1.0), writes=[("ones_f",)])
            stage = arena_f[:, 0:256]
            si = 0
            for nm, (o0, r) in (off.items() if 'noparam' not in os.environ.get('KDBG', '') else []):
                ap = Wd[nm]
                names = "abcdefg"[:len(ap.shape) - 1]
                flat = ap.rearrange("%s (c p) -> (%s c) p" % (" ".join(names), " ".join(names)), p=128)
                for r0 in range(0, r, 128):
                    rr = min(128, r - r0)
                    st = stage[:, (si % 2) * 128:(si % 2) * 128 + 128]
                    key = ("AF:stage", si % 2)
                    p.op("sp", lambda e, st=st, rr=rr, src=flat[r0:r0 + rr, :]: e.dma_start(out=st[0:rr, :], in_=src),
                         writes=[key], dma_sem="stg%d" % (si % 2))
                    b = p.bank()
                    p.op("pe", lambda e, b=b, st=st, rr=rr: e.transpose(PS[b][:, 0:rr], st[0:rr, :], ident_f[0:rr, 0:rr]),
                         reads=[key, ("ident_f",)], writes=[("ps", b)])
                    p.op("act", lambda e, b=b, rr=rr, c0=o0 + r0: e.activation(out=cols[:, c0:c0 + rr], in_=PS[b][:, 0:rr], func=AF.Copy),
                         reads=[("ps", b)], writes=[("cols",)])
                    si += 1
            p.op("act", lambda e: e.mul(acols[:, 0:192], cols[:, 0:192], ALPHA), reads=[("cols",)], writes=[("acols",)])
            for g in (range(4) if 'nopool' not in os.environ.get('KDBG', '') else []):
                win = 2 << g
                p.op("dve", lambda e, g=g, win=win: e.memset(invcnt[:, g, :], 1.0 / win), writes=[("invcnt",)])
                for t in range(win - 1):
                    p.op("dve", lambda e, g=g, t=t: e.memset(invcnt[:, g, t:t + 1], 1.0 / (t + 1)), writes=[("invcnt",)])
                p.op("pool", lambda e, g=g: e.dma_start(out=poolw[:, g], in_=Wd["pool_w"][0, g].rearrange("(i p) o -> p i o", p=128)),
                     writes=[("poolw",)], dma_sem="misc")
            if 'nogmlp' in os.environ.get('KDBG', ''):
                return
            p.op("sp", lambda e: e.dma_start(out=bsbc[:], in_=Wd["gmlp_bs"][0].partition_broadcast(128)),
                 writes=[("AF:bsbc",)], dma_sem="misc2")
            for h in range(8):
                st = stage[:, (si % 2) * 128:(si % 2) * 128 + 128]
                key = ("AF:stage", si % 2)
                p.op("sp", lambda e, st=st, h=h: e.dma_start(out=st, in_=Wd["gmlp_ws"][0, h]), writes=[key], dma_sem="stg%d" % (si % 2))
                p.op("pool", lambda e, st=st: e.affine_select(out=st, in_=st, compare_op=ALU.is_ge, fill=0.0, base=0,
                                                              pattern=[[-1, 128]], channel_multiplier=1),
                     reads=[key], writes=[key])
                b = p.bank()
                p.op("pe", lambda e, b=b, st=st: e.transpose(PS[b][:, 0:128], st, ident_f[:]),
                     reads=[key, ("ident_f",)], writes=[("ps", b)])
                p.op("act", lambda e, b=b, h=h: e.activation(out=wmt_f[:, h, :], in_=PS[b][:, 0:128], func=AF.Copy),
                     reads=[("ps", b)], writes=[("AF:wmt_f", h)])
                p.op("act", lambda e, b=b, h=h: e.activation(out=wmt_b[:, h, :], in_=PS[b][:, 0:128], func=AF.Copy),
                     reads=[("ps", b)], writes=[("wmt_b",)])
                b2 = p.bank()
                p.op("pe", lambda e, b2=b2, h=h: e.matmul(PS[b2][:, 0:128], ones_f[:], wmt_f[:, h, :], start=True, stop=True),
                     reads=[("AF:wmt_f", h), ("ones_f",)], writes=[("ps", b2)])
                for j in (2 * h, 2 * h + 1):
                    p.op("dve", lambda e, b2=b2, h=h, j=j: e.scalar_tensor_tensor(
                        out=cj[:, j, :], in0=PS[b2][:, 0:128], scalar=col("gmlp_v_ln_b", j), in1=bsbc[:, h, :],
                        op0=ALU.mult, op1=ALU.add), reads=[("ps", b2), ("cols",), ("AF:bsbc",)], writes=[("cj",)])
                si += 1

        def ln_tile_gen(srcs, src_keys, apply_fn):
            LAG = 2
            for ci in range(8 + LAG):
                if ci < 8:
                    c, r = ci, ci % 4
                    p.op("act", lambda e, c=c, r=r: e.activation(out=ysq_t[:, r, :], in_=srcs[c], func=AF.Square),
                         reads=[src_keys[c]], writes=[("ysq", r)])
                    p.op("dve", lambda e, c=c, r=r: e.tensor_copy(out=yb_t[:, r, :], in_=srcs[c]),
                         reads=[src_keys[c]], writes=[("yb", r)])
                if ci >= LAG:
                    c, r = ci - LAG, (ci - LAG) % 4

                    def fn(e, c=c, r=r):
                        e.matmul(PS[6][:], ones_b[:], yb_t[:, r, :], start=(c == 0), stop=(c == 7))
                        return e.matmul(PS[7][:], ones_b[:], ysq_t[:, r, :], start=(c == 0), stop=(c == 7))
                    p.op("pe", fn, reads=[("yb", r), ("ysq", r), ("ones_b",)], writes=[("ps", 6), ("ps", 7)])
                yield
            mean, tmp, rstd = lnm[:, 0, :], lnm[:, 1, :], lnm[:, 2, :]
            p.op("act", lambda e: e.activation(out=mean, in_=PS[6][:], func=AF.Copy), reads=[("ps", 6)], writes=[("lnm", 0)])
            p.op("act", lambda e: e.activation(out=tmp, in_=PS[6][:], func=AF.Square), reads=[("ps", 6)], writes=[("lnm", 1)])
            p.op("dve", lambda e: e.scalar_tensor_tensor(out=tmp, in0=PS[7][:], scalar=EPS, in1=tmp, op0=ALU.add, op1=ALU.subtract),
                 reads=[("ps", 7), ("lnm", 1)], writes=[("lnm", 1)])
            p.op("act", lambda e: e.activation(out=tmp, in_=tmp, func=AF.Sqrt), reads=[("lnm", 1)], writes=[("lnm", 1)])
            p.op("dve", lambda e: e.reciprocal(out=rstd, in_=tmp), reads=[("lnm", 1)], writes=[("lnm", 2)])
            yield
            for c0 in range(0, 8, 2):
                tbs = [(c, lnt[:, c % 4, :], ("lnt", c % 4)) for c in (c0, c0 + 1)]
                for c, tb, tk in tbs:
                    p.op("dve", lambda e, c=c, tb=tb: e.tensor_tensor(out=tb, in0=srcs[c], in1=mean, op=ALU.subtract),
                         reads=[src_keys[c], ("lnm", 0)], writes=[tk])
                for c, tb, tk in tbs:
                    p.op(LN_MUL_ENG, lambda e, tb=tb: e.tensor_tensor(out=tb, in0=tb, in1=rstd, op=ALU.mult),
                         reads=[tk, ("lnm", 2)], writes=[tk])
                for c, tb, tk in tbs:
                    apply_fn(c, tb, tk)
                yield

        def ln_tile(srcs, src_keys, apply_fn):
            p.drain()
            for _ in ln_tile_gen(srcs, src_keys, apply_fn):
                pass


        def ln_main(l, s, n, last):
            li = l * 3 + s
            sl = slice(n * TT, (n + 1) * TT)
            srcs = [xf[:, c, sl] for c in range(8)]
            keys = [("xf", c, n) for c in range(8)]

            def apply(c, tb, tk):
                gi = li * 8 + c
                g_ap, b_ap = col("ln_g", gi), col("ln_b", gi)
                if last:
                    p.op("act", lambda e: e.activation(out=xo[:, c, sl], in_=tb, func=AF.Identity, bias=b_ap, scale=g_ap),
                         reads=[tk, ("cols",)], writes=[("XO:x", c, n)])
                    return
                p.op("act", lambda e: e.activation(out=xb[:, c, sl], in_=tb, func=AF.Identity, bias=b_ap, scale=g_ap),
                     reads=[tk, ("cols",)], writes=[("xb", c, n)])
                ag, ab = acols[:, gi:gi + 1], acols[:, 96 + gi:96 + gi + 1]
                p.op("act", lambda e: e.activation(out=xf[:, c, sl], in_=tb, func=AF.Identity, bias=ab, scale=ag),
                     reads=[tk, ("cols",), ("acols",)], writes=[("xf", c, n)])
            if p.planning:
                return
            p.start_bg(ln_tile_gen(srcs, keys, apply), n, eager=1)

        def outproj(wsrc, nk_total, rhs_fn, rhs_keys_fn, tiles, first_bias=None, scale=None, after_tile=None):
            span = 8 if nk_total == 8 else 4
            blocks = [(k0, min(k0 + span, nk_total)) for k0 in range(0, nk_total, span)]
            for bi, (k0, k1) in enumerate(blocks):
                parts, tks_b = [], []
                for q0 in range(k0, k1, 4):
                    q1 = min(q0 + 4, k1)
                    slot, si, tk = wb.get([(lambda s, nk=q1 - q0: s[:, 0:nk, :], wsrc[:, q0:q1, :])])
                    parts.append((slot, si, q0, q1))
                    tks_b.append(tk)
                for n in tiles:
                    sl = slice(n * TT, (n + 1) * TT)
                    for c in range(8):
                        b = p.bank()
                        pairs, rk2 = [], []
                        for slot, si, q0, q1 in parts:
                            pairs += [(slot[:, kk - q0, c * 128:(c + 1) * 128], rhs_fn(kk, n)) for kk in range(q0, q1)]
                            rk2 += si + [rhs_keys_fn(kk, n) for kk in range(q0, q1)]
                        mm(PS[b][:], pairs, reads=rk2, bankkey=b)
                        if scale is not None:
                            sc = scale(c) if callable(scale) else scale
                            rk = [("cols",)] if callable(scale) else []
                            p.op("dve", lambda e, b=b, c=c, sl=sl, sc=sc: e.scalar_tensor_tensor(
                                out=xf[:, c, sl], in0=PS[b][:], scalar=sc, in1=xf[:, c, sl], op0=ALU.mult, op1=ALU.add),
                                reads=[("ps", b), ("xf", c, n)] + rk, writes=[("xf", c, n)])
                        elif first_bias is not None and bi == 0:
                            p.op("dve", lambda e, b=b, c=c, sl=sl: e.scalar_tensor_tensor(
                                out=xf[:, c, sl], in0=PS[b][:], scalar=first_bias(c), in1=xf[:, c, sl], op0=ALU.add, op1=ALU.add),
                                reads=[("ps", b), ("xf", c, n), ("cols",)], writes=[("xf", c, n)])
                        else:
                            p.op("dve", lambda e, b=b, c=c, sl=sl: e.tensor_tensor(out=xf[:, c, sl], in0=PS[b][:], in1=xf[:, c, sl], op=ALU.add),
                                 reads=[("ps", b), ("xf", c, n)], writes=[("xf", c, n)])
                    if bi == len(blocks) - 1 and after_tile is not None:
                        after_tile(n)
                for tk in tks_b:
                    wb.done(tk)

        def ffn(l, s, last):
            win = Wd["ffn_w_in"][l, s].rearrange("(k p) f -> p k f", p=128)
            wout = Wd["ffn_w_out"][l, s].rearrange("(j p) d -> p j d", p=128)
            p.fence("AB:")
            if last:
                p.fence("XO:")
                p.fence_merge("XO:", "AB:")
            hid = arena_b[:, 0:6 * T].rearrange("p (j t) -> p j t", j=6)
            blocks = [(0, 2), (2, 6), (6, 10), (10, 14), (14, 16), (16, 18), (18, 22)]
            wbs = {}
            cnt = [0]

            def p1(j, j0, jj, slot, si, n):
                sl = slice(n * TT, (n + 1) * TT)
                bg, bu = p.bank(), p.bank()
                xk = [("xb", k, n) for k in range(8)]
                mm(PS[bg][:], [(slot[:, k, jj * 128:(jj + 1) * 128], xb[:, k, sl]) for k in range(8)],
                   reads=si + xk, bankkey=bg)
                mm(PS[bu][:], [(slot[:, k, 256 + jj * 128:256 + (jj + 1) * 128], xb[:, k, sl]) for k in range(8)],
                   reads=si + xk, bankkey=bu)
                st = stt_t[:, cnt[0] % 2, :]
                sk = ("sttmp", cnt[0] % 2)
                cnt[0] += 1
                p.op("act", lambda e: e.activation(out=st, in_=PS[bg][:], func=AF.Silu), reads=[("ps", bg)], writes=[sk])
                p.op("dve", lambda e: e.tensor_tensor(out=hid[:, j - j0, sl], in0=PS[bu][:], in1=st, op=ALU.mult),
                     reads=[("ps", bu), sk], writes=[("AB:hid", j - j0, n)])

            def p2(parts, n):
                sl = slice(n * TT, (n + 1) * TT)
                for c in range(8):
                    b = p.bank()
                    pairs, rk, h0 = [], [], 0
                    for slot, si, nj in parts:
                        pairs += [(slot[:, jl, c * 128:(c + 1) * 128], hid[:, h0 + jl, sl]) for jl in range(nj)]
                        rk += si + [("AB:hid", h0 + jl, n) for jl in range(nj)]
                        h0 += nj
                    mm(PS[b][:], pairs, reads=rk, bankkey=b)
                    p.op("dve", lambda e, b=b, c=c: e.scalar_tensor_tensor(
                        out=xf[:, c, sl], in0=PS[b][:], scalar=0.5, in1=xf[:, c, sl], op0=ALU.mult, op1=ALU.add),
                        reads=[("ps", b), ("xf", c, n)], writes=[("xf", c, n)])

            groups = [(0, 6, True), (6, 10, False), (10, 14, False), (14, 16, False), (16, 22, True)]
            for gi, (j0, j1, tile_outer) in enumerate(groups):
                is_last = gi == len(groups) - 1
                tks = []
                for jp in range(j0, j1, 2):
                    slot, si, tk = wa.get([(lambda s_: s_[:, :, 0:256], win[:, :, jp * 128:jp * 128 + 256]),
                                           (lambda s_: s_[:, :, 256:512], win[:, :, DFF + jp * 128:DFF + jp * 128 + 256])])
                    tks.append((jp, slot, si, tk))
                parts, wtk = [], []
                for k0 in range(j0, j1, 4):
                    k1 = min(k0 + 4, j1)
                    slotb, sib, tkb = wb.get([(lambda s_, nk=k1 - k0: s_[:, 0:nk, :], wout[:, k0:k1, :])])
                    parts.append((slotb, sib, k1 - k0))
                    wtk.append(tkb)
                if tile_outer:
                    for n in range(NT):
                        for jp, slot, si, tk in tks:
                            for jj in range(2):
                                p1(jp + jj, j0, jj, slot, si, n)
                        p2(parts, n)
                        if is_last:
                            ln_main(l, 2 * s, n, last)
                else:
                    for n in range(NT):
                        for jp, slot, si, tk in tks:
                            for jj in range(2):
                                p1(jp + jj, j0, jj, slot, si, n)
                    for n in range(NT):
                        p2(parts, n)
                for jp, slot, si, tk in tks:
                    wa.done(tk)
                for tkb in wtk:
                    wb.done(tkb)

        def mixer_pool(l, half):
            p.fence("AB:")
            p.fence("XO:")
            p.fence_merge("AB:", "XO:")
            p.fence("AF:")
            Pb = arena_b[:, 0:8 * T].rearrange("p (c t) -> p c t", c=8)
            W = 16 + TT
            for n in range(NT):
                sl = slice(n * TT, (n + 1) * TT)
                for g in range(4):
                    win = 2 << g
                    pair = []
                    for c in (2 * g, 2 * g + 1):
                        base = (c % 2) * 2 * W
                        pair.append((c, arena_f[:, base:base + W], arena_f[:, base + W:base + 2 * W], ("AF:pa", c % 2), ("AF:pb", c % 2)))
                    for c, A, B, ka, kb in pair:
                        if n == 0:
                            if half == 0:
                                p.op("dve", lambda e, A=A: e.memset(A[:, 0:16], 0.0), writes=[ka])
                            else:
                                p.op("dve", lambda e, A=A, c=c: e.tensor_copy(out=A[:, 0:16], in_=pool_halo[:, c, :]),
                                     reads=[("pool_halo", c)], writes=[ka])
                            p.op("dve", lambda e, c=c: e.tensor_copy(out=pool_mid[:, c, :], in_=xb[:, c, TT - 16:TT]),
                                 reads=[("xb", c, 0)], writes=[("pool_mid", c)])
                        else:
                            p.op("dve", lambda e, A=A, c=c: e.tensor_copy(out=A[:, 0:16], in_=pool_mid[:, c, :]),
                                 reads=[("pool_mid", c)], writes=[ka])
                    for c, A, B, ka, kb in pair:
                        p.op("act", lambda e, A=A, c=c, sl=sl: e.activation(out=A[:, 16:W], in_=xb[:, c, sl], func=AF.Copy),
                             reads=[("xb", c, n), ka], writes=[ka])
                        if half == 0 and n == NT - 1:
                            p.op("dve", lambda e, c=c: e.tensor_copy(out=pool_halo[:, c, :], in_=xb[:, c, T - 16:T]),
                                 reads=[("xb", c, n)], writes=[("pool_halo", c)])
                    valid = 0
                    flip = False
                    for m in range(g + 1):
                        sh = 1 << m
                        for c, A, B, ka, kb in pair:
                            src, dst, ks, kd = (B, A, kb, ka) if flip else (A, B, ka, kb)
                            p.op("dve", lambda e, src=src, dst=dst, sh=sh, v0=valid: e.tensor_tensor(
                                out=dst[:, v0 + sh:W], in0=src[:, v0 + sh:W], in1=src[:, v0:W - sh], op=ALU.add),
                                reads=[ks], writes=[kd])
                        valid += sh
                        flip = not flip
                    for c, A, B, ka, kb in pair:
                        src, ks = (B, kb) if flip else (A, ka)
                        p.op("dve", lambda e, src=src, c=c, sl=sl, win=win: e.scalar_tensor_tensor(
                            out=Pb[:, c, sl], in0=src[:, 16:W], scalar=1.0 / win, in1=xb[:, c, sl], op0=ALU.mult, op1=ALU.subtract),
                            reads=[ks, ("xb", c, n)], writes=[("AB:P", c, n)])
                    if half == 0 and n == 0:
                        for c, A, B, ka, kb in pair:
                            src, ks = (B, kb) if flip else (A, ka)
                            sm = small[:, c % 2, :]
                            p.op("dve", lambda e, src=src, sm=sm, g=g: e.tensor_tensor(out=sm, in0=src[:, 16:32], in1=invcnt[:, g, :], op=ALU.mult),
                                 reads=[ks, ("invcnt",)], writes=[("small", c % 2)])
                        for c, A, B, ka, kb in pair:
                            sm = small[:, c % 2, :]
                            p.op("dve", lambda e, c=c, sm=sm: e.tensor_tensor(out=Pb[:, c, 0:16], in0=sm, in1=xb[:, c, 0:16], op=ALU.subtract),
                                 reads=[("small", c % 2), ("xb", c, 0)], writes=[("AB:P", c, 0)])
                    p.pump(4)
                for g in range(4):
                    for oc in range(2):
                        c = 2 * g + oc
                        b = p.bank()
                        mm(PS[b][:], [(poolw[:, g, ic, oc * 128:(oc + 1) * 128], Pb[:, 2 * g + ic, sl]) for ic in range(2)],
                           reads=[("poolw",)] + [("AB:P", 2 * g + ic, n) for ic in range(2)], bankkey=b)
                        p.op("dve", lambda e, b=b, c=c, sl=sl: e.scalar_tensor_tensor(
                            out=xf[:, c, sl], in0=PS[b][:], scalar=col("pool_scale", c), in1=xf[:, c, sl], op0=ALU.mult, op1=ALU.add),
                            reads=[("ps", b), ("xf", c, n), ("cols",)], writes=[("xf", c, n)])
                ln_main(l, 1, n, False)

        def mixer_sc(l, half):
            p.fence("AB:")
            p.fence("XO:")
            p.fence_merge("AB:", "XO:")
            p.fence("AF:")
            G = arena_b[:, 0:8 * T].rearrange("p (c t) -> p c t", c=8)
            win = Wd["sc_w_in"][0].rearrange("(k p) f -> p k f", p=128)
            ub = arena_f[:, 0:1056]
            Bb = arena_f[:, 1056:2080]
            vb = arena_f[:, 2080:3104]
            cnt = 0
            for c in range(8):
                slot, si, tk = wa.get([(lambda s_, q=q: s_[:, :, q * 128:(q + 1) * 128], win[:, :, q * 1024 + c * 128:q * 1024 + (c + 1) * 128])
                                       for q in range(3)])
                if half == 0:
                    p.op("dve", lambda e: e.memset(ub[:, 0:32], 0.0), writes=[("AF:ubh",)])
                else:
                    p.op("dve", lambda e, c=c: e.tensor_copy(out=ub[:, 30:32], in_=sc_halo[:, c, :]),
                         reads=[("sc_halo", c)], writes=[("AF:ubh",)])
                for n in range(NT):
                    sl = slice(n * TT, (n + 1) * TT)
                    bB, bC, bh = p.bank(), p.bank(), p.bank()
                    xk = [("xb", k, n) for k in range(8)]
                    for q, b in ((0, bB), (1, bC), (2, bh)):
                        mm(PS[b][:], [(slot[:, k, q * 128:(q + 1) * 128], xb[:, k, sl]) for k in range(8)],
                           reads=si + xk, bankkey=b)
                    st = stt_t[:, cnt % 2, :]
                    sk = ("sttmp", cnt % 2)
                    cnt += 1
                    p.op("act", lambda e, bC=bC, st=st: e.activation(out=st, in_=PS[bC][:], func=AF.Copy), reads=[("ps", bC)], writes=[sk])
                    p.op("dve", lambda e, bh=bh, st=st, n=n: e.tensor_tensor(out=ub[:, 32 + n * TT:32 + (n + 1) * TT], in0=PS[bh][:], in1=st, op=ALU.mult),
                         reads=[("ps", bh), sk], writes=[("AF:ub", n)])
                    p.op("act", lambda e, bB=bB, sl=sl: e.activation(out=Bb[:, sl], in_=PS[bB][:], func=AF.Copy),
                         reads=[("ps", bB)], writes=[("AF:Bb", n)])
                wa.done(tk)
                ukeys = [("AF:ub", 0), ("AF:ub", 1), ("AF:ubh",)]
                p.op("dve", lambda e, c=c: e.tensor_scalar(out=vb, in0=ub[:, 32:1056], scalar1=col("sc_conv", 16 + c), scalar2=None, op0=ALU.mult),
                     reads=ukeys + [("cols",)], writes=[("AF:vb",)])
                p.op("dve", lambda e, c=c: e.scalar_tensor_tensor(out=vb, in0=ub[:, 31:1055], scalar=col("sc_conv", 8 + c), in1=vb, op0=ALU.mult, op1=ALU.add),
                     reads=ukeys + [("cols",), ("AF:vb",)], writes=[("AF:vb",)])
                p.op("dve", lambda e, c=c: e.scalar_tensor_tensor(out=vb, in0=ub[:, 30:1054], scalar=col("sc_conv", c), in1=vb, op0=ALU.mult, op1=ALU.add),
                     reads=ukeys + [("cols",), ("AF:vb",)], writes=[("AF:vb",)])
                p.op("dve", lambda e, c=c: e.tensor_tensor(out=G[:, c, :], in0=vb, in1=Bb, op=ALU.mult),
                     reads=[("AF:vb",), ("AF:Bb", 0), ("AF:Bb", 1)], writes=[("AB:G", c, 0), ("AB:G", c, 1)])
                if half == 0:
                    p.op("dve", lambda e, c=c: e.tensor_copy(out=sc_halo[:, c, :], in_=ub[:, 1054:1056]),
                         reads=[("AF:ub", 1)], writes=[("sc_halo", c)])
            wo = Wd["sc_w_out"][0].rearrange("(k p) d -> p k d", p=128)
            outproj(wo, 8, lambda k, n: G[:, k, n * TT:(n + 1) * TT], lambda k, n: ("AB:G", k, n), range(NT),
                    after_tile=lambda n: ln_main(l, 1, n, False))

        def mixer_conv(l, half):
            p.fence("AB:")
            p.fence("XO:")
            p.fence_merge("AB:", "XO:")
            p.fence("AF:")
            hg = arena_b[:, 0:8 * 1056].rearrange("p (c t) -> p c t", c=8)
            hn = arena_b[:, 8448:8448 + 8 * T].rearrange("p (c t) -> p c t", c=8)
            dg = arena_b[:, 16640:16640 + 2 * 31 * 128].rearrange("p (a k m) -> p a k m", a=2, k=31)
            hc = arena_f[:, 0:4096].rearrange("p (c t) -> p c t", c=8)
            win = Wd["conv_w_in"][0].rearrange("(k p) f -> p k f", p=128)
            cnt = 0
            tks = []
            for c2 in range(4):
                slot, si, tk = wa.get([(lambda s_: s_[:, :, 0:256], win[:, :, c2 * 256:(c2 + 1) * 256]),
                                       (lambda s_: s_[:, :, 256:512], win[:, :, 1024 + c2 * 256:1024 + (c2 + 1) * 256])])
                tks.append((slot, si, tk))
            for n in range(NT):
                sl = slice(n * TT, (n + 1) * TT)
                for c2 in range(4):
                    slot, si, tk = tks[c2]
                    for jj in range(2):
                        c = 2 * c2 + jj
                        if n == 0:
                            if half == 0:
                                p.op("dve", lambda e, c=c: e.memset(hg[:, c, 0:32], 0.0), writes=[("AB:hgh", c)])
                            else:
                                p.op("dve", lambda e, c=c: e.tensor_copy(out=hg[:, c, 0:32], in_=conv_halo[:, c, :]),
                                     reads=[("conv_halo", c)], writes=[("AB:hgh", c)])
                        ba, bg = p.bank(), p.bank()
                        xk = [("xb", k, n) for k in range(8)]
                        mm(PS[ba][:], [(slot[:, k, jj * 128:(jj + 1) * 128], xb[:, k, sl]) for k in range(8)], reads=si + xk, bankkey=ba)
                        mm(PS[bg][:], [(slot[:, k, 256 + jj * 128:256 + (jj + 1) * 128], xb[:, k, sl]) for k in range(8)], reads=si + xk, bankkey=bg)
                        st = stt_t[:, cnt % 2, :]
                        sk = ("sttmp", cnt % 2)
                        cnt += 1
                        p.op("act", lambda e, bg=bg, st=st, c=c: e.activation(out=st, in_=PS[bg][:], func=AF.Sigmoid, bias=col("conv_b_in", 8 + c)),
                             reads=[("ps", bg), ("cols",)], writes=[sk])
                        p.op("dve", lambda e, ba=ba, st=st, c=c, n=n: e.scalar_tensor_tensor(
                            out=hg[:, c, 32 + n * TT:32 + (n + 1) * TT], in0=PS[ba][:], scalar=col("conv_b_in", c), in1=st, op0=ALU.add, op1=ALU.mult),
                            reads=[("ps", ba), sk, ("cols",)], writes=[("AB:hg", c, n)])
                        if half == 0 and n == NT - 1:
                            p.op("dve", lambda e, c=c: e.tensor_copy(out=conv_halo[:, c, :], in_=hg[:, c, T:T + 32]),
                                 reads=[("AB:hg", c, 1)], writes=[("conv_halo", c)])
            for slot, si, tk in tks:
                wa.done(tk)
            dwo = off["conv_dw"][0]
            di = 0
            for n in range(NT):
                for c in range(8):
                    a = di % 2
                    di += 1
                    p.op("dve", lambda e, a=a, c=c: e.tensor_tensor(
                        out=dg[:, a], in0=ident_b[:].unsqueeze(1).broadcast_to([128, 31, 128]),
                        in1=cols[:, dwo + c:dwo + 248:8].unsqueeze(2).broadcast_to([128, 31, 128]), op=ALU.mult),
                        reads=[("ident_b",), ("cols",)], writes=[("AB:dg", a)])
                    b = p.bank()
                    rk = [("AB:dg", a), ("AB:hg", c, n)] + ([("AB:hg", c, n - 1)] if n > 0 else [("AB:hgh", c)])
                    mm(PS[b][:], [(dg[:, a, k, :], hg[:, c, 2 + k + n * TT:2 + k + (n + 1) * TT]) for k in range(31)], reads=rk, bankkey=b)
                    p.op("act", lambda e, b=b, c=c: e.activation(out=hc[:, c, :], in_=PS[b][:], func=AF.Identity, bias=col("conv_dw_b", c)),
                         reads=[("ps", b), ("cols",)], writes=[("AF:hc", c)])

                def apply(c, tb, tk, n=n):
                    p.op("act", lambda e: e.activation(out=hn[:, c, n * TT:(n + 1) * TT], in_=tb, func=AF.Silu,
                                                       bias=col("conv_ln_b", c), scale=col("conv_ln_g", c)),
                         reads=[tk, ("cols",)], writes=[("AB:hn", c, n)])
                ln_tile([hc[:, c, :] for c in range(8)], [("AF:hc", c) for c in range(8)], apply)
            wo = Wd["conv_w_out"][0].rearrange("(k p) d -> p k d", p=128)
            outproj(wo, 8, lambda k, n: hn[:, k, n * TT:(n + 1) * TT], lambda k, n: ("AB:hn", k, n), range(NT),
                    first_bias=lambda c: col("conv_b_out", c), after_tile=lambda n: ln_main(l, 1, n, False))

        def mixer_gmlp(l, half):
            p.fence("AB:")
            p.fence("XO:")
            p.fence_merge("AB:", "XO:")
            p.fence("AF:")
            wv = arena_b[:, 0:16384].rearrange("p (k e) -> p k e", k=8)
            u = arena_b[:, 16384:24576].rearrange("p (j t) -> p j t", j=16)
            vtm2 = arena_f[:, 0:4096].rearrange("p (a e) -> p a e", a=2)
            vn1 = ysq_t[:, :, :].rearrange("p r t -> p (r t)")
            win = Wd["gmlp_w_in"][0].rearrange("(k p) f -> p k f", p=128)
            for q in range(4):
                p.op("pool", lambda e, q=q: e.dma_start(out=wv[:, :, q * 512:(q + 1) * 512], in_=win[:, :, 2048 + q * 512:2048 + (q + 1) * 512]),
                     writes=[("AB:wv", q)], dma_sem="wv%d" % q)
            wo = Wd["gmlp_w_out"][0].rearrange("(k p) d -> p k d", p=128)
            vi = 0
            for n in range(NT):
                sl = slice(n * TT, (n + 1) * TT)
                xk = [("xb", k, n) for k in range(8)]
                for e4 in range(4):
                    slot, si, tk = wa.get([(lambda s_: s_[:, :, :], win[:, :, e4 * 512:(e4 + 1) * 512])])
                    for jj in range(4):
                        j = e4 * 4 + jj
                        b = p.bank()
                        mm(PS[b][:], [(slot[:, k, jj * 128:(jj + 1) * 128], xb[:, k, sl]) for k in range(8)], reads=si + xk, bankkey=b)
                        p.op("act", lambda e, b=b, j=j: e.activation(out=u[:, j, :], in_=PS[b][:], func=AF.Gelu_apprx_tanh),
                             reads=[("ps", b)], writes=[("AB:u", j)])
                    wa.done(tk)
                p.drain()
                vnk = [("ysq", r) for r in range(4)]

                def v_stage(tb):
                    t0 = n * TT + tb * 128
                    a = tb % 2
                    vt = vtm2[:, a, :]
                    bv = [p.bank() for _ in range(4)]
                    stats = gst[:, a, 0:24]
                    mv = gst[:, a, 24:26]
                    sd = gst[:, a, 26:27]
                    rstd = gst[:, a, 27:28]
                    nmr = gst[:, a, 28:29]
                    for et in range(4):
                        mm(PS[bv[et]][:], [(xb[:, k, t0:t0 + 128], wv[:, k, et * 512:(et + 1) * 512]) for k in range(8)],
                           reads=xk + [("AB:wv", et)], bankkey=bv[et])
                        p.op("act", lambda e, et=et, b=bv[et]: e.activation(out=vt[:, et * 512:(et + 1) * 512], in_=PS[b][:], func=AF.Gelu_apprx_tanh),
                             reads=[("ps", bv[et])], writes=[("AF:vtm", a, et)])
                        p.op("dve", lambda e, et=et: e.bn_stats(out=stats[:, et * 6:(et + 1) * 6], in_=vt[:, et * 512:(et + 1) * 512]),
                             reads=[("AF:vtm", a, et)], writes=[("gst", a, et)])
                    p.op("dve", lambda e: e.bn_aggr(out=mv, in_=stats),
                         reads=[("gst", a, et) for et in range(4)], writes=[("gmv", a)])
                    p.op("dve", lambda e: e.tensor_scalar(out=sd, in0=mv[:, 1:2], scalar1=EPS, scalar2=None, op0=ALU.add),
                         reads=[("gmv", a)], writes=[("gsd", a)])
                    p.op("act", lambda e: e.activation(out=sd, in_=sd, func=AF.Sqrt), reads=[("gsd", a)], writes=[("gsd", a)])
                    p.op("dve", lambda e: e.reciprocal(out=rstd, in_=sd), reads=[("gsd", a)], writes=[("grs", a)])
                    p.op("dve", lambda e: e.tensor_scalar(out=nmr, in0=mv[:, 0:1], scalar1=-1.0, scalar2=rstd, op0=ALU.mult, op1=ALU.mult),
                         reads=[("gmv", a), ("grs", a)], writes=[("gnm", a)])

                def norm(tb):
                    a = tb % 2
                    p.op("act", lambda e: e.activation(out=vn1, in_=vtm2[:, a, :], func=AF.Identity, bias=gst[:, a, 28:29], scale=gst[:, a, 27:28]),
                         reads=[("AF:vtm", a, et) for et in range(4)] + [("grs", a), ("gnm", a)], writes=vnk)

                def s_stage(tb):
                    si2 = 0
                    for j4 in range(4):
                        b = p.bank()

                        def fn(e, b=b, j4=j4):
                            ins = None
                            for q in range(4):
                                j = j4 * 4 + q
                                ins = e.matmul(PS[b][:, q * 128:(q + 1) * 128], vn1[:, j * 128:(j + 1) * 128], wmt_b[:, j // 2, :], start=True, stop=True)
                            return ins
                        p.op("pe", fn, reads=vnk + [("wmt_b",)], writes=[("ps", b)])
                        for q in range(4):
                            j = j4 * 4 + q
                            sv = svt[:, si2 % 2, :]
                            svk = ("svt", si2 % 2)
                            si2 += 1
                            p.op("dve", lambda e, b=b, q=q, j=j, sv=sv: e.scalar_tensor_tensor(
                                out=sv, in0=PS[b][:, q * 128:(q + 1) * 128], scalar=col("gmlp_v_ln_g", j), in1=cj[:, j, :], op0=ALU.mult, op1=ALU.add),
                                reads=[("ps", b), ("cols",), ("cj",)], writes=[svk])
                            p.op("dve", lambda e, j=j, sv=sv: e.tensor_tensor(
                                out=u[:, j, tb * 128:(tb + 1) * 128], in0=sv, in1=u[:, j, tb * 128:(tb + 1) * 128], op=ALU.mult),
                                reads=[svk, ("AB:u", j)], writes=[("AB:u", j)])

                v_stage(0)
                norm(0)
                for tb in range(4):
                    if tb + 1 < 4:
                        v_stage(tb + 1)
                    s_stage(tb)
                    if tb + 1 < 4:
                        norm(tb + 1)
                outproj(wo, 16, lambda k, n_: u[:, k, :], lambda k, n_: ("AB:u", k), [n],
                        after_tile=lambda n_: ln_main(l, 1, n_, False))

        NTB = T // 128
        ld_issued = set()

        def ld_stage(tb):
            return arena_f[:, 2048 + (tb % 2) * 1024:2048 + (tb % 2 + 1) * 1024], ("AF:ld", tb % 2)

        def load_dma(uidx, tb):
            if (uidx, tb) in ld_issued or uidx >= NU or tb >= NTB:
                return
            ld_issued.add((uidx, tb))
            st, key = ld_stage(tb)
            r0 = uidx * T + tb * 128
            p.op("sp", lambda e: e.dma_start(out=st, in_=x_d[r0:r0 + 128, :]), writes=[key], dma_sem="io%d" % (tb % 2))

        def load_tb(uidx, tb):
            load_dma(uidx, tb)
            st, key = ld_stage(tb)
            n = tb // 4
            for cg in range(2):
                b = p.bank()

                def fn(e, b=b, cg=cg):
                    ins = None
                    for q in range(4):
                        cc = cg * 4 + q
                        ins = e.transpose(PS[b][:, q * 128:(q + 1) * 128], st[:, cc * 128:(cc + 1) * 128], ident_f[:])
                    return ins
                p.op("pe", fn, reads=[key, ("ident_f",)], writes=[("ps", b)])
                cs = slice(tb * 128, (tb + 1) * 128)
                if cg == 0:
                    for q in range(4):
                        p.op("act", lambda e, b=b, q=q: e.activation(out=xf[:, q, cs], in_=PS[b][:, q * 128:(q + 1) * 128], func=AF.Copy, scale=ALPHA),
                             reads=[("ps", b)], writes=[("xf", q, n)])
                        p.op("act", lambda e, b=b, q=q: e.activation(out=xb[:, q, cs], in_=PS[b][:, q * 128:(q + 1) * 128], func=AF.Copy),
                             reads=[("ps", b)], writes=[("xb", q, n)])
                else:
                    psv = PS[b][:].rearrange("p (q t) -> p q t", q=4)
                    p.op("dve", lambda e, psv=psv: e.tensor_scalar(out=xf[:, 4:8, cs], in0=psv, scalar1=ALPHA, scalar2=None, op0=ALU.mult),
                         reads=[("ps", b)], writes=[("xf", 4 + q, n) for q in range(4)])
                    p.op("dve", lambda e, psv=psv: e.tensor_copy(out=xb[:, 4:8, cs], in_=psv),
                         reads=[("ps", b)], writes=[("xb", 4 + q, n) for q in range(4)])
            load_dma(uidx, tb + 2)

        out_handles = {}

        def store_tb(uidx, tb):
            st = arena_f[:, (tb % 2) * 1024:(tb % 2 + 1) * 1024]
            key = ("AF:st", tb % 2)
            n = tb // 4
            for cg in range(2):
                b = p.bank()

                def fn(e, b=b, cg=cg):
                    ins = None
                    for q in range(4):
                        cc = cg * 4 + q
                        ins = e.transpose(PS[b][:, q * 128:(q + 1) * 128], xo[:, cc, tb * 128:(tb + 1) * 128], ident_f[:])
                    return ins
                p.op("pe", fn, reads=[("XO:x", cg * 4 + q, n) for q in range(4)] + [("ident_f",)], writes=[("ps", b)])
                if cg == 0:
                    p.op("act", lambda e, b=b: e.activation(out=st[:, 0:512], in_=PS[b][:], func=AF.Copy), reads=[("ps", b)], writes=[key])
                else:
                    p.op("dve", lambda e, b=b: e.tensor_copy(out=st[:, 512:1024], in_=PS[b][:]), reads=[("ps", b)], writes=[key])
            r0 = uidx * T + tb * 128
            h = p.op("sp", lambda e: e.dma_start(out=out_d[r0:r0 + 128, :], in_=st), reads=[key], dma_sem="oo%d" % (tb % 2))
            if h:
                out_handles.update(h)

        mixers = [mixer_pool, mixer_gmlp, mixer_conv, mixer_sc]

        def program():
            ld_issued.clear()
            setup()
            p.fence("AF:")
            for tb in range(NTB):
                load_tb(0, tb)
            for uidx in range(NU):
                half = uidx % 2
                for l in range(nlayers):
                    last = (l == nlayers - 1)
                    nsub = 3 if sub_stop is None or l < nlayers - 1 else sub_stop
                    if nsub >= 1:
                        ffn(l, 0, False)
                    if nsub >= 2:
                        mixers[l](l, half)
                    if nsub >= 3:
                        if last:
                            p.fence("AF:")
                            load_dma(uidx + 1, 0)
                            load_dma(uidx + 1, 1)
                        ffn(l, 1, last)
                if nlayers == 0 or sub_stop not in (None, 3):
                    raise NotImplementedError("final LN -> xo is required (run with full last layer)")
                for tb in range(NTB):
                    store_tb(uidx, tb)
                    if uidx + 1 < NU:
                        load_tb(uidx + 1, tb)
                    p.pump(3)
            p.drain()

        p.planning = True
        program()
        p.planning = False
        wa.reset()
        wb.reset()
        p.nbank = 0
        program()

        block = es.enter_context(nc.Block())

        @block.tensor
        def _(e):
            p.replay("pe", e)

        @block.scalar
        def _(e):
            p.replay("act", e)

        @block.vector
        def _(e):
            p.replay("dve", e)

        @block.gpsimd
        def _(e):
            p.replay("pool", e)

        @block.sync
        def _(e):
            p.replay("sp", e)
            p.final_waits("sp", e, out_handles)
    build_nc.last_stats = {e: len(q) for e, q in p.q.items()}
    return nc


def kernel(**inputs):
    x = np.ascontiguousarray(inputs["x"], dtype=np.float32)
    B, S, _ = x.shape
    per = B // N_CORES
    NU = per * S // T
    nc = build_nc(NU)
    weights = {name: np.ascontiguousarray(inputs[name], dtype=np.float32) for name, _ in WSHAPES}
    in_maps = []
    for i in range(N_CORES):
        m = {"x": x[i * per:(i + 1) * per].reshape(per * S, D)}
        m.update(weights)
        in_maps.append(m)
    res = run_bass_kernel_spmd(nc, in_maps, core_ids=list(range(N_CORES)))
    outs = [np.asarray(r["out"]).reshape(per, S, D) for r in res.results]
    return np.concatenate(outs, axis=0).astype(np.float32, copy=False)
```
